# Optimizing a Trainium2 kernel written in Bass

```python
import math
import jax, jax.numpy as jnp
from jax import lax
import numpy as np

D_MODEL = 1024
BATCH = 2
SEQ = 8192
DEPTH = 1

GRID_W = 64
HEAD_DIM_A = 128
N_Q_HEADS_A = 8
N_KV_HEADS_A = 2
ROPE_THETA = 10000.0
Q_BLOCK = 128
HEAD_DIM_B = 64
N_HEADS_PER_DIL = 4
DIL_PAIRS = ((128, 1), (512, 4), (2048, 16))
N_HEADS_B = N_HEADS_PER_DIL * len(DIL_PAIRS)
BAND_BLOCK = 64
N_REL_BUCKETS = 32
REL_MAX_DIST = 1024
D_FF = 4 * D_MODEL
D_PLE = 256
NORM_EPS = 1e-6
NEG_INF = -1e30

SPLIT_SIZES = (
    N_Q_HEADS_A * HEAD_DIM_A,
    N_KV_HEADS_A * HEAD_DIM_A,
    N_KV_HEADS_A * HEAD_DIM_A,
    N_HEADS_B * HEAD_DIM_B,
    N_HEADS_B * HEAD_DIM_B,
    N_HEADS_B * HEAD_DIM_B,
    D_MODEL,
    D_MODEL,
)
D_IN_PROJ = sum(SPLIT_SIZES)

kernel_name = "hybrid_gqa_axialrope_dilated_swa_sqrelu_ple"


def rmsnorm(x, g):
    xf = x.astype(jnp.float32)
    y = xf * lax.rsqrt(jnp.mean(xf * xf, axis=-1, keepdims=True) + NORM_EPS)
    return (y * g.astype(jnp.float32)).astype(x.dtype)


def rope_1d(x, pos):
    d = x.shape[-1]
    inv_freq = jnp.power(ROPE_THETA, -jnp.arange(0, d, 2, dtype=jnp.float32) / d)
    ang = pos.astype(jnp.float32)[:, None] * inv_freq[None, :]
    cos = jnp.cos(ang)[None, :, None, :]
    sin = jnp.sin(ang)[None, :, None, :]
    xf = x.astype(jnp.float32)
    x1, x2 = xf[..., : d // 2], xf[..., d // 2:]
    out = jnp.concatenate([x1 * cos - x2 * sin, x2 * cos + x1 * sin], axis=-1)
    return out.astype(x.dtype)


def axial_rope(x, row_ids, col_ids):
    half = x.shape[-1] // 2
    return jnp.concatenate([rope_1d(x[..., :half], row_ids),
                            rope_1d(x[..., half:], col_ids)], axis=-1)


def t5_bucket(rel):
    nb = N_REL_BUCKETS // 2
    ret = (rel > 0).astype(np.int32) * nb
    n = np.abs(rel)
    max_exact = nb // 2
    large = max_exact + (np.log(np.maximum(n, 1) / max_exact)
                         / math.log(REL_MAX_DIST / max_exact)
                         * (nb - max_exact)).astype(np.int32)
    large = np.minimum(large, nb - 1)
    return ret + np.where(n < max_exact, n, large).astype(np.int32)


def mixer_a(q, k, v, q_g, k_g, row_ids, col_ids):
    b, s, hq, hd = q.shape
    hkv = k.shape[2]
    grp = hq // hkv
    q = axial_rope(rmsnorm(q, q_g), row_ids, col_ids) * (hd ** -0.5)
    k = axial_rope(rmsnorm(k, k_g), row_ids, col_ids)
    nblk = s // Q_BLOCK
    qb = q.reshape(b, nblk, Q_BLOCK, hkv, grp, hd).transpose(1, 0, 3, 4, 2, 5)
    kt = k.transpose(0, 2, 1, 3)
    vt = v.transpose(0, 2, 1, 3)

    def attend(qblk):
        sc = jnp.einsum('bkgqd,bksd->bkgqs', qblk, kt).astype(jnp.float32)
        pr = jax.nn.softmax(sc, axis=-1).astype(vt.dtype)
        return jnp.einsum('bkgqs,bksd->bkgqd', pr, vt)

    o = lax.map(attend, qb)
    return o.transpose(1, 0, 4, 2, 3, 5).reshape(b, s, hq * hd)


def dilated_group(q, k, v, bias_table, window, dilation):
    b, s, hh, hd = q.shape
    L = s // dilation
    W = window // (2 * dilation)
    qb_len = math.gcd(L, BAND_BLOCK)
    nb = L // qb_len
    kw_len = qb_len + 2 * W

    def to_sub(t):
        return t.reshape(b, L, dilation, hh, hd).transpose(0, 2, 3, 1, 4)

    off = np.arange(kw_len)[None, :] - W - np.arange(qb_len)[:, None]
    band = np.abs(off) <= W
    kpos = np.arange(nb)[:, None] * qb_len - W + np.arange(kw_len)[None, :]
    inb = (kpos >= 0) & (kpos < L)
    mask = band[None, :, :] & inb[:, None, :]
    bucket = t5_bucket(off * dilation)
    bias = bias_table[bucket].astype(jnp.float32).transpose(2, 0, 1)
    idx = np.arange(nb)[:, None] * qb_len + np.arange(kw_len)[None, :]

    qs = to_sub(q).reshape(b, dilation, hh, nb, qb_len, hd) * (hd ** -0.5)
    pad = ((0, 0), (0, 0), (0, 0), (W, W), (0, 0))
    ks = jnp.pad(to_sub(k), pad)[:, :, :, idx, :]
    vs = jnp.pad(to_sub(v), pad)[:, :, :, idx, :]
    sc = jnp.einsum('bchnqd,bchnkd->bchnqk', qs, ks).astype(jnp.float32) + bias[:, None]
    sc = jnp.where(mask, sc, NEG_INF)
    lse = jax.nn.logsumexp(sc, axis=-1)
    pr = jnp.exp(sc - lse[..., None]).astype(vs.dtype)
    o = jnp.einsum('bchnqk,bchnkd->bchnqd', pr, vs)
    o = o.reshape(b, dilation, hh, L, hd).transpose(0, 3, 1, 2, 4).reshape(b, s, hh, hd)
    lse = lse.reshape(b, dilation, hh, L).transpose(0, 3, 1, 2).reshape(b, s, hh)
    return o, lse


def mixer_b(q, k, v, rel_bias):
    b, s, _, hd = q.shape
    outs, lses = [], []
    for g, (window, dilation) in enumerate(DIL_PAIRS):
        sl = slice(g * N_HEADS_PER_DIL, (g + 1) * N_HEADS_PER_DIL)
        o, l = dilated_group(q[:, :, sl], k[:, :, sl], v[:, :, sl], rel_bias[:, sl], window, dilation)
        outs.append(o)
        lses.append(l)
    wts = jax.nn.softmax(jnp.stack(lses, axis=0), axis=0).astype(q.dtype)
    o = jnp.sum(wts[..., None] * jnp.stack(outs, axis=0), axis=0)
    return o.reshape(b, s, N_HEADS_PER_DIL * hd)


def setup_inputs(seed: int = 0) -> dict:
    key = jax.random.key(seed)
    ks = jax.random.split(key, 20)
    f32 = jnp.float32

    def w(k, shape, fan_in):
        return jax.random.normal(k, shape, f32) * (fan_in ** -0.5)

    def gain(k, shape):
        return 1.0 + 0.05 * jax.random.normal(k, shape, f32)

    return {
        "x": jax.random.normal(ks[0], (BATCH, SEQ, D_MODEL), f32),
        "p": jax.random.normal(ks[1], (DEPTH, BATCH, SEQ, D_PLE), f32),
        "norm_mix_g": gain(ks[2], (DEPTH, D_MODEL)),
        "w_in": w(ks[3], (DEPTH, D_MODEL, D_IN_PROJ), D_MODEL),
        "b_gate": 0.02 * jax.random.normal(ks[4], (DEPTH, 2 * D_MODEL), f32),
        "q_norm_g": gain(ks[5], (DEPTH, HEAD_DIM_A)),
        "k_norm_g": gain(ks[6], (DEPTH, HEAD_DIM_A)),
        "rel_bias": 0.5 * jax.random.normal(ks[7], (N_REL_BUCKETS, N_HEADS_B), f32),
        "w_out_a": w(ks[8], (DEPTH, N_Q_HEADS_A * HEAD_DIM_A, D_MODEL), N_Q_HEADS_A * HEAD_DIM_A),
        "w_out_b": w(ks[9], (DEPTH, N_HEADS_PER_DIL * HEAD_DIM_B, D_MODEL), N_HEADS_PER_DIL * HEAD_DIM_B),
        "w_out": w(ks[10], (DEPTH, D_MODEL, D_MODEL), D_MODEL),
        "norm_mlp_g": gain(ks[11], (DEPTH, D_MODEL)),
        "w_ff1": w(ks[12], (DEPTH, D_MODEL, D_FF), D_MODEL),
        "w_ff2": w(ks[13], (DEPTH, D_FF, D_MODEL), D_FF),
        "norm_ple_g": gain(ks[14], (DEPTH, D_MODEL)),
        "w_ple_gate": w(ks[15], (DEPTH, D_MODEL, D_MODEL), D_MODEL),
        "w_ple": w(ks[16], (DEPTH, D_PLE, D_MODEL), D_PLE),
        "final_norm_g": gain(ks[17], (D_MODEL,)),
    }


def reference(x, p, norm_mix_g, w_in, b_gate, q_norm_g, k_norm_g, rel_bias,
              w_out_a, w_out_b, w_out, norm_mlp_g, w_ff1, w_ff2,
              norm_ple_g, w_ple_gate, w_ple, final_norm_g):
    b, s, _ = x.shape
    rows = s // GRID_W
    row_ids = jnp.repeat(jnp.arange(rows, dtype=jnp.int32), GRID_W)
    col_ids = jnp.arange(s, dtype=jnp.int32) % GRID_W
    split_at = list(np.cumsum(SPLIT_SIZES)[:-1])

    for i in range(DEPTH):
        h = rmsnorm(x, norm_mix_g[i])
        z = h @ w_in[i]
        qa, ka, va, qb, kb, vb, ga, gb = jnp.split(z, split_at, axis=-1)
        qa = qa.reshape(b, s, N_Q_HEADS_A, HEAD_DIM_A)
        ka = ka.reshape(b, s, N_KV_HEADS_A, HEAD_DIM_A)
        va = va.reshape(b, s, N_KV_HEADS_A, HEAD_DIM_A)
        qb = qb.reshape(b, s, N_HEADS_B, HEAD_DIM_B)
        kb = kb.reshape(b, s, N_HEADS_B, HEAD_DIM_B)
        vb = vb.reshape(b, s, N_HEADS_B, HEAD_DIM_B)

        y_a = mixer_a(qa, ka, va, q_norm_g[i], k_norm_g[i], row_ids, col_ids) @ w_out_a[i]
        y_b = mixer_b(qb, kb, vb, rel_bias) @ w_out_b[i]
        gate_a = jax.nn.sigmoid(ga + b_gate[i, :D_MODEL])
        gate_b = jax.nn.sigmoid(gb + b_gate[i, D_MODEL:])
        x = x + (gate_a * y_a + gate_b * y_b) @ w_out[i]

        h = rmsnorm(x, norm_mlp_g[i])
        x = x + jnp.square(jax.nn.relu(h @ w_ff1[i])) @ w_ff2[i]

        gate_p = jax.nn.sigmoid(rmsnorm(x, norm_ple_g[i]) @ w_ple_gate[i])
        x = x + gate_p * (p[i] @ w_ple[i])

    return rmsnorm(x, final_norm_g)
```

```python
import math
from contextlib import ExitStack
import numpy as np
import concourse.bass as bass
import concourse.mybir as mybir
from concourse.bass_utils import run_bass_kernel_spmd

F32 = mybir.dt.float32
BF16 = mybir.dt.bfloat16
AF = mybir.ActivationFunctionType
ALU = mybir.AluOpType

ENGS = ("pe", "act", "dve", "pool", "sp")
S = 8192
D = 1024
NOWN = 2048
NEG = -100.0
EPS = 1e-6
DIL = (1, 4, 16)
NJ = (17, 5, 2)
GOFF = (0, 17, 37)


class Buf:
    __slots__ = ("name", "w", "rd", "dsem", "dcnt")

    def __init__(self, name):
        self.name = name
        self.w = None
        self.rd = {}
        self.dsem = None
        self.dcnt = 0


class Ins:
    __slots__ = ("eng", "fn", "waits", "flag", "dbuf", "seq")

    def __init__(self, eng, fn, seq):
        self.eng = eng
        self.fn = fn
        self.waits = []
        self.flag = False
        self.dbuf = None
        self.seq = seq


class Prog:
    def __init__(self, nc):
        self.nc = nc
        self.q = {e: [] for e in ENGS}
        self.waited = {e: {} for e in ENGS}
        self.dma_bufs = []

    def _need(self, ins, tok):
        if tok is None:
            return
        if tok[0] == "e":
            _, eng, seq = tok
            if eng == "pe" and ins.eng == "pe":
                return
            key = ("e", eng)
            val = seq
        else:
            _, buf, cnt = tok
            key = ("d", id(buf))
            val = cnt
        w = self.waited[ins.eng]
        if w.get(key, -1) >= val:
            return
        w[key] = val
        ins.waits.append(tok)
        if tok[0] == "e":
            self.q[tok[1]][tok[2]].flag = True

    def _deps(self, ins, reads, writes):
        for b in reads:
            self._need(ins, b.w)
        for b in writes:
            self._need(ins, b.w)
            for eng, seq in b.rd.items():
                self._need(ins, ("e", eng, seq))
            if b.dcnt:
                self._need(ins, ("d", b, b.dcnt))

    def op(self, eng, fn, reads=(), writes=()):
        ins = Ins(eng, fn, len(self.q[eng]))
        self._deps(ins, reads, writes)
        self.q[eng].append(ins)
        tok = ("e", eng, ins.seq)
        for b in reads:
            b.rd[eng] = ins.seq
        for b in writes:
            b.w = tok
            b.rd = {}
        return ins

    def dma(self, queue, out, in_, track, is_write, reads=(), writes=()):
        ins = Ins(queue, (lambda e: e.dma_start(out=out, in_=in_)), len(self.q[queue]))
        rs = list(reads)
        ws = list(writes)
        (ws if is_write else rs).append(track)
        self._deps(ins, rs, ws)
        if track.dsem is None:
            track.dsem = True
            self.dma_bufs.append(track)
        track.dcnt += 1
        ins.dbuf = track
        self.q[queue].append(ins)
        if is_write:
            track.w = ("d", track, track.dcnt)
            track.rd = {}
        return ins

    def barrier(self):
        last = {}
        for e in ("pe", "act", "dve", "pool"):
            s = len(self.q[e]) - 1
            while s >= 0 and (self.q[e][s].dbuf is not None or self.q[e][s].fn is None):
                s -= 1
            last[e] = s
        dl = [(b, b.dcnt) for b in self.dma_bufs if b.dcnt]
        for e in ENGS:
            ins = Ins(e, None, len(self.q[e]))
            for pe_, s in last.items():
                if s >= 0:
                    self._need(ins, ("e", pe_, s))
            for b, c in dl:
                self._need(ins, ("d", b, c))
            self.q[e].append(ins)

    def emit(self, stack):
        nc = self.nc
        esem = {e: stack.enter_context(nc.semaphore("ms_" + e)) for e in ("pe", "act", "dve", "pool")}
        for i, b in enumerate(self.dma_bufs):
            b.dsem = stack.enter_context(nc.semaphore("d%d_%s" % (i, b.name)))
        rank = {}
        for e in ("pe", "act", "dve", "pool"):
            r = 0
            for ins in self.q[e]:
                if ins.flag:
                    r += 1
                    rank[(e, ins.seq)] = r
        q = self.q

        def run(e, h):
            for ins in q[e]:
                for tok in ins.waits:
                    if tok[0] == "e":
                        h.wait_ge(esem[tok[1]], rank[(tok[1], tok[2])])
                    else:
                        h.wait_ge(tok[1].dsem, 16 * tok[2])
                if ins.fn is None:
                    continue
                bi = ins.fn(h)
                if ins.dbuf is not None:
                    bi.then_inc(ins.dbuf.dsem, 16)
                elif ins.flag:
                    bi.then_inc(esem[e], 1)

        block = stack.enter_context(nc.Block())

        @block.tensor
        def _(t):
            run("pe", t)

        @block.scalar
        def _(t):
            run("act", t)

        @block.vector
        def _(t):
            run("dve", t)

        @block.gpsimd
        def _(t):
            run("pool", t)

        @block.sync
        def _(t):
            run("sp", t)


class Arena:
    def __init__(self, nc, nwords):
        self.ap = nc.alloc_sbuf_tensor("arena", [128, nwords], F32).ap()
        self.free = [(0, nwords)]
        self.live = {}

    def alloc(self, name, shape, dt, top=False):
        n = int(np.prod(shape[1:]))
        nbytes = n * (4 if dt == F32 else 2)
        w = (nbytes + 31) // 32 * 8
        order = range(len(self.free) - 1, -1, -1) if top else range(len(self.free))
        for i in order:
            o, sz = self.free[i]
            if sz >= w:
                if top:
                    off = o + sz - w
                    if sz == w:
                        self.free.pop(i)
                    else:
                        self.free[i] = (o, sz - w)
                else:
                    off = o
                    if sz == w:
                        self.free.pop(i)
                    else:
                        self.free[i] = (o + w, sz - w)
                break
        else:
            raise RuntimeError("arena full allocating %s (%d words); free=%s" % (name, w, self.free))
        assert name not in self.live, name
        self.live[name] = (off, w)
        a = self.ap[0:shape[0], off:off + w]
        if dt != F32:
            a = a.bitcast(dt)
        a = a[:, 0:n]
        if len(shape) == 3:
            a = a.rearrange("p (a b) -> p a b", a=shape[1])
        elif len(shape) == 4:
            a = a.rearrange("p (a b c) -> p a b c", a=shape[1], b=shape[2])
        return a

    def release(self, *names):
        for name in names:
            off, w = self.live.pop(name)
            self.free.append((off, w))
        self.free.sort()
        m = []
        for o, sz in self.free:
            if m and m[-1][0] + m[-1][1] == o:
                m[-1] = (m[-1][0], m[-1][1] + sz)
            else:
                m.append((o, sz))
        self.free = m


def build(stage="full"):
    nc = bass.Bass("TRN2", target_bir_lowering=False)

    def din(name, shape):
        return nc.dram_tensor(name, list(shape), F32, kind="ExternalInput").ap()

    xs = din("xs", [S, D])
    xh = din("xh", [4096, D])
    pp = din("pp", [NOWN, 256])
    rope = din("rope", [S, 256])
    kmask_d = din("kmask", [128, 69])
    biasT_d = din("biasT", [6, 128, 512])
    w_in = din("w_in", [D, 5888])
    w_oa = din("w_out_a", [D, D])
    w_ob = din("w_out_b", [256, D])
    w_o = din("w_out", [D, D])
    w_f1 = din("w_ff1", [D, 4096])
    w_f2 = din("w_ff2", [4096, D])
    w_pg = din("w_ple_gate", [D, D])
    w_p = din("w_ple", [256, D])
    g_mix = din("g_mix", [1, D])
    g_mlp = din("g_mlp", [1, D])
    g_ple = din("g_ple", [1, D])
    g_fin = din("g_fin", [1, D])
    g_q = din("g_q", [1, 128])
    g_k = din("g_k", [1, 128])
    bgate_d = din("bgate", [128, 16])
    out = nc.dram_tensor("out", [NOWN, D], F32, kind="ExternalOutput").ap()
    vscr = nc.dram_tensor("vscr", [4096, 768], BF16).ap()
    dbg = None
    if stage == "B":
        dbg = nc.dram_tensor("dbg", [128, 2 * NOWN], F32, kind="ExternalOutput").ap()
    elif stage == "A":
        dbg = nc.dram_tensor("dbg", [128, 8 * NOWN], F32, kind="ExternalOutput").ap()

    w_in_v = w_in.rearrange("(k p) n -> p k n", p=128)

    st = ExitStack()
    with st:
        P = Prog(nc)
        AR = Arena(nc, 50400)
        ps = st.enter_context(nc.psum_tensor("ps", [128, 4096], F32)).ap()
        PB = [Buf("ps%d" % i) for i in range(8)]

        def bank(i, n=512):
            return ps[:, i * 512:i * 512 + n]

        def bankb(i, n=1024):
            return ps[:, i * 512:(i + 1) * 512].bitcast(BF16)[:, 0:n]

        def MM(o, lhsT, rhs, start, stop, reads, writes):
            P.op("pe", lambda e: e.matmul(o, lhsT, rhs, start=start, stop=stop), reads, writes)

        def TR(o, i, reads, writes):
            P.op("pe", lambda e: e.transpose(o, i, ident), list(reads) + [B_ident], writes)

        def ACTV(o, i, func, reads, writes, bias=0.0, scale=1.0, accum=None):
            P.op("act", lambda e: e.activation(o, i, func, bias=bias, scale=scale, accum_out=accum), reads, writes)

        def TT(eng, o, a, b, op, reads, writes):
            P.op(eng, lambda e: e.tensor_tensor(o, a, b, op), reads, writes)

        def STT(eng, o, a, sc, b, op0, op1, reads, writes):
            P.op(eng, lambda e: e.scalar_tensor_tensor(o, a, sc, b, op0, op1), reads, writes)

        def CP(eng, o, i, reads, writes):
            if eng == "act":
                P.op("act", lambda e: e.copy(o, i), reads, writes)
            else:
                P.op(eng, lambda e: e.tensor_copy(o, i), reads, writes)

        def RECIP(o, i, reads, writes):
            P.op("dve", lambda e: e.reciprocal(o, i), reads, writes)

        def MEMSET(eng, o, v, writes):
            P.op(eng, lambda e: e.memset(o, v), (), writes)

        def LOADW(dst, src, buf):
            for k in range(dst.shape[1]):
                P.dma("pool", dst[:, k, :], src[:, k, :], buf, True)

        class Ring:
            def __init__(self, name, n, shape, dt, top=False):
                self.aps = [AR.alloc("%s%d" % (name, i), shape, dt, top=top) for i in range(n)]
                self.bufs = [Buf("%s%d" % (name, i)) for i in range(n)]
                self.names = ["%s%d" % (name, i) for i in range(n)]
                self.i = 0

            def next(self):
                k = self.i % len(self.aps)
                self.i += 1
                return self.aps[k], self.bufs[k]

            def release(self):
                AR.release(*self.names)

        ident = AR.alloc("ident", [128, 128], BF16)
        B_ident = Buf("ident")
        identf = AR.alloc("identf", [128, 128], F32)
        B_identf = Buf("identf")
        MEMSET("pool", identf, 0.0, [B_identf])
        P.op("pool", lambda e: e.affine_select(identf, identf, [[-1, 128]], ALU.not_equal, 1.0, base=0,
                                               channel_multiplier=1), [B_identf], [B_identf])
        CP("dve", ident, identf, [B_identf], [B_ident])
        ones = AR.alloc("ones", [128, 128], BF16)
        B_ones = Buf("ones")
        MEMSET("pool", ones, 1.0, [B_ones])
        gq_bc = AR.alloc("gq_bc", [128, 128], F32)
        gk_bc = AR.alloc("gk_bc", [128, 128], F32)
        B_gq = Buf("gq")
        B_gk = Buf("gk")
        P.dma("sp", gq_bc, g_q.partition_broadcast(128), B_gq, True)
        P.dma("sp", gk_bc, g_k.partition_broadcast(128), B_gk, True)
        bgate = AR.alloc("bgate", [128, 16], F32)
        B_bgate = Buf("bgate")
        P.dma("sp", bgate, bgate_d, B_bgate, True)
        gbc = AR.alloc("gbc", [128, D], F32)
        B_gbc = Buf("gbc")
        P.dma("sp", gbc, g_mix.partition_broadcast(128), B_gbc, True)
        stat = AR.alloc("stat", [128, 128], F32)
        sqj = AR.alloc("sqj", [128, D], F32)
        B_sqj = Buf("sqj")
        statbufs = [Buf("stat%d" % i) for i in range(32)]
        stat_i = [0]

        def stat_slot(n=1):
            assert n <= 4
            k = stat_i[0] % 32
            stat_i[0] += 1
            return stat[:, 4 * k:4 * k + n], statbufs[k]

        def rms_rstd(x_ap, Bx, width, ncols=1, xs_list=None):
            ss, Bss = stat_slot(ncols)
            srcs = xs_list if xs_list is not None else [x_ap]
            for c, src in enumerate(srcs):
                ACTV(sqj[:, 0:width], src, AF.Square, [Bx], [B_sqj, Bss], accum=ss[:, c:c + 1])
            rs, Brs = stat_slot(ncols)
            ACTV(rs, ss, AF.Sqrt, [Bss], [Brs], bias=EPS, scale=1.0 / width)
            RECIP(rs, rs, [Brs], [Brs])
            return rs, Brs

        def norm_to_hT(x_ap, Bx, hb_ring, trbank, dst_ap, Bdst, copy_eng):
            rs, Brs = rms_rstd(x_ap, Bx, D)
            hb, Bhb = hb_ring.next()
            STT("dve", hb, x_ap, rs[:, 0:1], gbc, ALU.mult, ALU.mult, [Bx, Brs, B_gbc], [Bhb])
            pt = bankb(trbank)
            for k in range(8):
                TR(pt[:, k * 128:(k + 1) * 128], hb[:, k * 128:(k + 1) * 128], [Bhb], [PB[trbank]])
            CP(copy_eng, dst_ap, pt.rearrange("p (k n) -> p k n", k=8), [PB[trbank]], [Bdst])

        KBT = AR.alloc("KBT", [128, 6, 4096], BF16)
        QBT = AR.alloc("QBT", [128, 6, NOWN], BF16)
        B_KBT = Buf("KBT")
        B_QBT = Buf("QBT")
        WB = AR.alloc("WB", [128, 8, 2304], BF16)
        B_WB = Buf("WB")
        LOADW(WB, w_in_v[:, :, 1536:3840], B_WB)
        xring = Ring("xr", 3, [128, D], F32)
        hbring = Ring("hb", 2, [128, D], BF16)
        hTg = Ring("hTg", 2, [128, 8, 512], BF16)
        vtr = Ring("vt", 2, [128, 768], BF16)
        for G in range(0 if stage[0] == "x" else (8 if stage != "1a1" else 1)):
            hT, BhT = hTg.next()
            for t in range(4):
                T = G * 4 + t
                xa, Bx = xring.next()
                P.dma("sp", xa, xh[T * 128:(T + 1) * 128, :], Bx, True)
                norm_to_hT(xa, Bx, hbring, T % 2, hT[:, :, t * 128:(t + 1) * 128], BhT, "act")
            for c in range(6):
                bk = 2 + (c % 2)
                for k in range(8):
                    MM(bank(bk), WB[:, k, 768 + c * 128:768 + (c + 1) * 128], hT[:, k, :], k == 0, k == 7,
                       [B_WB, BhT], [PB[bk]])
                CP("dve", KBT[:, c, G * 512:(G + 1) * 512], bank(bk), [PB[bk]], [B_KBT])
            if 2 <= G < 6:
                for c in range(6):
                    bk = 2 + (c % 2)
                    for k in range(8):
                        MM(bank(bk), WB[:, k, c * 128:(c + 1) * 128], hT[:, k, :], k == 0, k == 7,
                           [B_WB, BhT], [PB[bk]])
                    ACTV(QBT[:, c, (G - 2) * 512:(G - 1) * 512], bank(bk), AF.Copy, [PB[bk]], [B_QBT], scale=0.125)
            for t in range(4):
                T = G * 4 + t
                b0 = 4 + 2 * (t % 2)
                for k in range(8):
                    MM(bank(b0), hT[:, k, t * 128:(t + 1) * 128], WB[:, k, 1536:2048], k == 0, k == 7,
                       [B_WB, BhT], [PB[b0]])
                for k in range(8):
                    MM(bank(b0 + 1, 256), hT[:, k, t * 128:(t + 1) * 128], WB[:, k, 2048:2304], k == 0, k == 7,
                       [B_WB, BhT], [PB[b0 + 1]])
                vt, Bvt = vtr.next()
                CP("dve", vt[:, 0:512], bank(b0), [PB[b0]], [Bvt])
                CP("act", vt[:, 512:768], bank(b0 + 1, 256), [PB[b0 + 1]], [Bvt])
                P.dma("sp", vscr[T * 128:(T + 1) * 128, :], vt, Bvt, False)
        P.barrier()
        if stage in ("1a", "1a1"):
            P.emit(st)
            return nc
        AR.release("WB")
        xring.release()
        hbring.release()
        hTg.release()
        vtr.release()

        BT = AR.alloc("BT", [128, 2, NOWN], BF16, top=True)
        B_BT = Buf("BT")
        Vext = AR.alloc("Vext", [128, 32, 4, 128], BF16)
        B_Vext = Buf("Vext")
        Vraw = AR.alloc("Vraw", [128, 32, 256], BF16)
        B_Vraw = Buf("Vraw")
        Bacc = AR.alloc("Bacc", [128, 4, NOWN], F32)
        B_Bacc = Buf("Bacc")
        kmask = AR.alloc("kmask", [128, 69], F32)
        B_kmask = Buf("kmask")
        P.dma("sp", kmask, kmask_d, B_kmask, True)
        biasr = Ring("bias", 2, [128, 2, 512], F32)
        s2r = Ring("s2", 2, [128, 512], F32)
        ptr = Ring("ptb", 3, [128, 512], BF16)
        Vext3 = Vext.rearrange("p j h n -> p (j h) n")
        MEMSET("pool", Vext3[:, :, 64:128], 1.0, [B_Vext])
        qpr = Ring("qpad", 3, [128, 2, 256], BF16)
        for qa_, qb_ in zip(qpr.aps, qpr.bufs):
            MEMSET("pool", qa_, 0.0, [qb_])
        nblk = 0
        for g in range(3):
            c = DIL[g]
            nj = NJ[g]
            Lq = NOWN // c
            bias_ap, Bbias = biasr.next()
            P.dma("sp", bias_ap, biasT_d[2 * g:2 * g + 2].rearrange("k p n -> p k n"), Bbias, True)
            for r in range(c):
                base = 1024 + r - 64 * c
                for j0 in range(0, nj, 4):
                    j1 = min(nj, j0 + 4)
                    b0_ = base + c * 128 * j0
                    src = vscr[b0_:b0_ + c * (128 * (j1 - j0) - 1) + 1:c, g * 256:(g + 1) * 256].rearrange(
                        "(j p) n -> p j n", p=128)
                    P.dma("sp", Vraw[:, r * nj + j0:r * nj + j1, :], src, B_Vraw, True)
            CP("pool", Vext3[:, 0:c * nj * 4, 0:64],
               Vraw[:, 0:c * nj, :].rearrange("p j (h d) -> p (j h) d", h=4), [B_Vraw], [B_Vext])
            xl = int(stage[1]) if stage[0] == "x" else 9
            for r in range(c):
                if stage == "1b_a" or (stage in ("1b_b",) and g > 0) or (stage[0] == "x" and g > 0):
                    break
                for i in range(Lq // 128 if stage[0] != "x" else 2):
                    sb = [2 * (nblk % 2), 2 * (nblk % 2) + 1]
                    ob = 4 + (nblk % 2)
                    nblk += 1
                    pts = []
                    qstart = r + c * 128 * i
                    qp, Bqp = qpr.next()
                    for cc in range(2):
                        ch = 2 * g + cc
                        CP("act", qp[0:64, cc, 0:128], QBT[0:64, ch, qstart:qstart + c * 127 + 1:c], [B_QBT], [Bqp])
                        CP("act", qp[64:128, cc, 128:256], QBT[64:128, ch, qstart:qstart + c * 127 + 1:c], [B_QBT], [Bqp])
                    for kind in range(2):
                        j = i + kind
                        kstart = 1024 + r + c * (128 * j - 64)
                        for cc in range(2):
                            ch = 2 * g + cc
                            MM(bank(sb[kind])[:, cc * 256:(cc + 1) * 256],
                               KBT[:, ch, kstart:kstart + c * 127 + 1:c], qp[:, cc, :], True, True,
                               [B_KBT, Bqp], [PB[sb[kind]]])
                        if xl < 2:
                            continue
                        s2, Bs2 = s2r.next()
                        TT("dve", s2, bank(sb[kind]), bias_ap[:, kind, :], ALU.add, [PB[sb[kind]], Bbias], [Bs2])
                        pt, Bpt = ptr.next()
                        tile_idx = GOFF[g] + r * nj + j
                        ACTV(pt, s2, AF.Exp, [Bs2, B_kmask], [Bpt], bias=kmask[:, tile_idx:tile_idx + 1])
                        pts.append((pt, Bpt))
                    if xl < 3:
                        continue
                    for hh in range(4):
                        for kind in range(2):
                            j = i + kind
                            slot = r * nj + j
                            lhsT = Vext[:, slot, hh, :]
                            MM(bank(ob)[:, hh * 128:(hh + 1) * 128], lhsT, pts[kind][0][:, hh * 128:(hh + 1) * 128],
                               kind == 0, kind == 1, [B_Vext, pts[kind][1]], [PB[ob]])
                    if xl < 4:
                        continue
                    qstart = r + c * 128 * i
                    dst = Bacc[:, :, qstart:qstart + c * 127 + 1:c]
                    srcp = bank(ob).rearrange("p (h n) -> p h n", h=4)
                    if g == 0:
                        CP("dve", dst, srcp, [PB[ob]], [B_Bacc])
                    else:
                        TT("dve", dst, srcp, dst, ALU.add, [PB[ob], B_Bacc], [B_Bacc])
        P.barrier()
        if stage in ("1b_a", "1b_b", "B2") or stage[0] == "x":
            P.emit(st)
            return nc
        AR.release("Vext", "Vraw")
        denlo = AR.alloc("denlo", [64, 4, NOWN], F32)
        B_denlo = Buf("denlo")
        for hh in range(4):
            P.dma("sp", denlo[:, hh, :], Bacc[64:128, hh, :], B_denlo, True, reads=[B_Bacc])
        BTo = AR.alloc("BTo", [64, 2, NOWN], BF16)
        B_BTo = Buf("BTo")
        for hh in range(4):
            RECIP(denlo[:, hh, :], denlo[:, hh, :], [B_denlo], [B_denlo])
            if hh % 2 == 0:
                TT("dve", BT[0:64, hh // 2, :], Bacc[0:64, hh, :], denlo[:, hh, :], ALU.mult, [B_Bacc, B_denlo], [B_BT])
            else:
                TT("dve", BTo[:, hh // 2, :], Bacc[0:64, hh, :], denlo[:, hh, :], ALU.mult, [B_Bacc, B_denlo], [B_BTo])
        for k2 in range(2):
            P.dma("sp", BT[64:128, k2, :], BTo[:, k2, :], B_BT, True, reads=[B_BTo])
        if stage == "B":
            P.barrier()
            AR.release("Bacc")
            dtmp = AR.alloc("dtmpB", [128, 2 * NOWN], F32)
            B_dtmp = Buf("dtmpB")
            CP("dve", dtmp, BT.rearrange("p a n -> p (a n)"), [B_BT], [B_dtmp])
            P.dma("sp", dbg[:, 0:2048], dtmp[:, 0:2048], B_dtmp, False)
            P.dma("sp", dbg[:, 2048:4096], dtmp[:, 2048:4096], B_dtmp, False)
            P.barrier()
            P.emit(st)
            return nc
        P.barrier()
        AR.release("denlo", "Bacc", "kmask", "KBT", "QBT", "BTo")
        qpr.release()
        biasr.release()
        s2r.release()
        ptr.release()

        KAT = AR.alloc("KAT", [128, 2, S], BF16)
        VA = AR.alloc("VA", [128, 64, 256], BF16)
        QAT = AR.alloc("QAT", [128, 8, NOWN], BF16)
        B_KAT = [Buf("KAT%d" % i) for i in range(16)]
        B_VA = [Buf("VA%d" % i) for i in range(16)]
        B_QAT = [Buf("QAT%d" % i) for i in range(4)]
        WA = AR.alloc("WA", [128, 8, 1536], BF16)
        B_WA = Buf("WA")
        LOADW(WA, w_in_v[:, :, 0:1536], B_WA)
        xring = Ring("xr", 3, [128, D], F32)
        rpr = Ring("rp", 3, [128, 256], F32)
        hbring = Ring("hb", 2, [128, D], BF16)
        hTr = Ring("hTt", 2, [128, 8, 128], BF16)
        knr = Ring("kn", 2, [128, 512], F32)
        t1r = Ring("t1", 2, [128, 512], F32)
        t2r = Ring("t2", 2, [128, 512], F32)
        krr = Ring("kr", 2, [128, 512], BF16)

        def qk_post(src_bank, Bsrc, nh, g_bc, Bg, rp, Brp, dst_fn):
            W = nh * 128
            rs, Brs = rms_rstd(None, Bsrc, 128, ncols=nh, xs_list=[src_bank[:, h * 128:(h + 1) * 128] for h in range(nh)])
            kn, Bkn = knr.next()
            for h in range(nh):
                STT("dve", kn[:, h * 128:(h + 1) * 128], src_bank[:, h * 128:(h + 1) * 128], rs[:, h:h + 1], g_bc,
                    ALU.mult, ALU.mult, [Bsrc, Brs, Bg], [Bkn])
            t1, Bt1 = t1r.next()
            t2, Bt2 = t2r.next()
            knv = kn[:, 0:W].rearrange("p (h n) -> p h n", h=nh)
            Cb = rp[:, 0:128].unsqueeze(1).to_broadcast([128, nh, 128])
            TT("pool", t1[:, 0:W].rearrange("p (h n) -> p h n", h=nh), knv, Cb, ALU.mult, [Bkn, Brp], [Bt1])
            kn5 = kn[:, 0:W].rearrange("p (h a b d) -> p h a b d", h=nh, a=2, b=2)
            t25 = t2[:, 0:W].rearrange("p (h a b d) -> p h a b d", h=nh, a=2, b=2)
            S5 = rp[:, 128:256].rearrange("p (a b d) -> p a b d", a=2, b=2)
            for b_ in range(2):
                for a_ in range(2):
                    Sb = S5[:, a_, b_, :].unsqueeze(1).to_broadcast([128, nh, 32])
                    TT("pool", t25[:, :, a_, b_, :], kn5[:, :, a_, 1 - b_, :], Sb, ALU.mult, [Bkn, Brp], [Bt2])
            kr, Bkr = krr.next()
            TT("dve", kr[:, 0:W], t1[:, 0:W], t2[:, 0:W], ALU.add, [Bt1, Bt2], [Bkr])
            return kr, Bkr

        for T in range(64):
            xa, Bx = xring.next()
            P.dma("sp", xa, xs[T * 128:(T + 1) * 128, :], Bx, True)
            rp, Brp = rpr.next()
            P.dma("sp", rp, rope[T * 128:(T + 1) * 128, :], Brp, True)
            hT, BhT = hTr.next()
            norm_to_hT(xa, Bx, hbring, T % 2, hT, BhT, "act")
            bkv = 2 + (T % 2)
            for k in range(8):
                MM(bank(bkv), hT[:, k, :], WA[:, k, 1024:1536], k == 0, k == 7, [BhT, B_WA], [PB[bkv]])
            own = T < 16
            if own:
                for hq in range(2):
                    for k in range(8):
                        MM(bank(4 + hq), hT[:, k, :], WA[:, k, hq * 512:(hq + 1) * 512], k == 0, k == 7,
                           [BhT, B_WA], [PB[4 + hq]])
            CP("act", VA[:, T, :], bank(bkv)[:, 256:512], [PB[bkv]], [B_VA[T // 4]])
            kr, Bkr = qk_post(bank(bkv)[:, 0:256], PB[bkv], 2, gk_bc, B_gk, rp, Brp, None)
            ptk = bankb(6)
            for h in range(2):
                TR(ptk[:, h * 128:(h + 1) * 128], kr[:, h * 128:(h + 1) * 128], [Bkr], [PB[6]])
            CP("act", KAT[:, :, T * 128:(T + 1) * 128], ptk[:, 0:256].rearrange("p (h n) -> p h n", h=2),
               [PB[6]], [B_KAT[T // 4]])
            if own:
                ptq = bankb(7)
                for hq in range(2):
                    qr, Bqr = qk_post(bank(4 + hq), PB[4 + hq], 4, gq_bc, B_gq, rp, Brp, None)
                    for h in range(4):
                        TR(ptq[:, (hq * 4 + h) * 128:(hq * 4 + h + 1) * 128], qr[:, h * 128:(h + 1) * 128], [Bqr], [PB[7]])
                CP("dve", QAT[:, :, T * 128:(T + 1) * 128], ptq.rearrange("p (h n) -> p h n", h=8),
                   [PB[7]], [B_QAT[T // 4]])
        P.barrier()
        if stage == "P2":
            P.emit(st)
            return nc
        AR.release("WA")
        for r_ in (xring, rpr, hbring, hTr, knr, t1r, t2r, krr):
            r_.release()

        AT = AR.alloc("AT", [128, 8, NOWN], BF16, top=True)
        B_AT = [Buf("AT%d" % i) for i in range(4)]
        ptr = Ring("ptA", 3, [128, 1024], BF16)
        rdr = Ring("rden", 2, [128, 512], F32)
        scale_a = 128.0 ** -0.5
        for qg in range(4):
            for h in range(8):
                kv = h // 4
                it = qg * 8 + h
                ob = 4 + (it % 2)
                db = 6 + (it % 2)
                q_ap = QAT[:, h, qg * 512:(qg + 1) * 512]

                def qk(jp):
                    sb0 = 2 * (jp % 2)
                    for u in range(2):
                        kt = 2 * jp + u
                        MM(bank(sb0 + u), KAT[:, kv, kt * 128:(kt + 1) * 128], q_ap, True, True,
                           [B_KAT[kt // 4], B_QAT[qg]], [PB[sb0 + u]])

                def ex(jp):
                    sb0 = 2 * (jp % 2)
                    pt, Bpt = ptr.next()
                    ACTV(pt, ps[:, sb0 * 512:(sb0 + 2) * 512], AF.Exp, [PB[sb0], PB[sb0 + 1]], [Bpt], scale=scale_a)
                    return pt, Bpt

                def pv(jp, pt, Bpt):
                    for u in range(2):
                        kt = 2 * jp + u
                        MM(bank(ob), VA[:, kt, kv * 128:(kv + 1) * 128], pt[:, u * 512:(u + 1) * 512],
                           kt == 0, kt == 63, [B_VA[kt // 4], Bpt], [PB[ob]])
                        MM(bank(db), ones, pt[:, u * 512:(u + 1) * 512], kt == 0, kt == 63,
                           [B_ones, Bpt], [PB[db]])

                qk(0)
                for jp in range(32):
                    if jp + 1 < 32:
                        qk(jp + 1)
                    pt, Bpt = ex(jp)
                    pv(jp, pt, Bpt)
                rd, Brd = rdr.next()
                RECIP(rd, bank(db), [PB[db]], [Brd])
                TT("dve", AT[:, h, qg * 512:(qg + 1) * 512], bank(ob), rd, ALU.mult, [PB[ob], Brd], [B_AT[qg]])
        if stage == "A":
            P.barrier()
            AR.release("KAT", "VA")
            dtmp = AR.alloc("dtmp", [128, 8 * NOWN], F32)
            B_dtmp = Buf("dtmp")
            CP("dve", dtmp, AT.rearrange("p a n -> p (a n)"), B_AT, [B_dtmp])
            P.dma("sp", dbg, dtmp, B_dtmp, False)
            P.barrier()
            P.emit(st)
            return nc
        P.barrier()
        AR.release("KAT", "VA", "QAT")
        ptr.release()
        rdr.release()

        Wg = AR.alloc("Wg", [128, 8, 2048], BF16)
        WoA = AR.alloc("WoA", [128, 8, D], BF16)
        WoB = AR.alloc("WoB", [128, 2, D], BF16)
        B_Wg, B_WoA, B_WoB = Buf("Wg"), Buf("WoA"), Buf("WoB")
        LOADW(Wg, w_in_v[:, :, 3840:5888], B_Wg)
        LOADW(WoA, w_oa.rearrange("(k p) n -> p k n", p=128), B_WoA)
        LOADW(WoB, w_ob.rearrange("(k p) n -> p k n", p=128), B_WoB)
        xring = Ring("xr", 2, [128, D], F32)
        hbring = Ring("hb", 2, [128, D], BF16, top=True)
        hTg = Ring("hTg", 2, [128, 8, 512], BF16)
        Gtmp = AR.alloc("Gtmp", [128, 8, 512], BF16)
        B_Gtmp = Buf("Gtmp")
        sar = Ring("sa", 2, [128, 512], F32)
        sbr = Ring("sb", 2, [128, 512], F32)
        m1r = Ring("m1", 2, [128, 512], F32)
        m2r = Ring("m2", 2, [128, 512], F32)
        for grp in range(4):
            hT, BhT = hTg.next()
            for t in range(4):
                T = grp * 4 + t
                xa, Bx = xring.next()
                P.dma("sp", xa, xs[T * 128:(T + 1) * 128, :], Bx, True)
                norm_to_hT(xa, Bx, hbring, 0, hT[:, :, t * 128:(t + 1) * 128], BhT, "act")
            tok = slice(grp * 512, (grp + 1) * 512)
            for c in range(8):
                a = c % 2
                bga, bgb, bya, byb = 1 + a, 7, 3 + a, 5 + a
                cs = slice(c * 128, (c + 1) * 128)
                for k in range(8):
                    MM(bank(bga), Wg[:, k, cs], hT[:, k, :], k == 0, k == 7, [B_Wg, BhT], [PB[bga]])
                sa, Bsa = sar.next()
                ACTV(sa, bank(bga), AF.Sigmoid, [PB[bga], B_bgate], [Bsa], bias=bgate[:, c:c + 1])
                for k in range(8):
                    MM(bank(bgb), Wg[:, k, 1024 + c * 128:1024 + (c + 1) * 128], hT[:, k, :], k == 0, k == 7,
                       [B_Wg, BhT], [PB[bgb]])
                sb_, Bsb = sbr.next()
                ACTV(sb_, bank(bgb), AF.Sigmoid, [PB[bgb], B_bgate], [Bsb], bias=bgate[:, 8 + c:9 + c])
                for k in range(8):
                    MM(bank(bya), WoA[:, k, cs], AT[:, k, tok], k == 0, k == 7, [B_WoA, B_AT[grp]], [PB[bya]])
                for k in range(2):
                    MM(bank(byb), WoB[:, k, cs], BT[:, k, tok], k == 0, k == 1, [B_WoB, B_BT], [PB[byb]])
                m1, Bm1 = m1r.next()
                m2, Bm2 = m2r.next()
                TT("dve", m1, bank(bya), sa, ALU.mult, [PB[bya], Bsa], [Bm1])
                TT("dve", m2, bank(byb), sb_, ALU.mult, [PB[byb], Bsb], [Bm2])
                TT("pool", Gtmp[:, c, :], m1, m2, ALU.add, [Bm1, Bm2], [B_Gtmp])
            CP("pool", AT[:, :, tok], Gtmp, [B_Gtmp], [B_AT[grp]])
        P.barrier()
        AR.release("Wg", "WoA", "WoB", "Gtmp", "BT")
        for r_ in (xring, hTg, sar, sbr, m1r, m2r):
            r_.release()

        R = AR.alloc("R", [128, 16, D], F32)
        B_R = [Buf("R%d" % i) for i in range(16)]
        WO = AR.alloc("WO", [128, 8, D], BF16)
        B_WO = Buf("WO")
        LOADW(WO, w_o.rearrange("(k p) n -> p k n", p=128), B_WO)
        for T in range(16):
            P.dma("sp", R[:, T, :], xs[T * 128:(T + 1) * 128, :], B_R[T], True)
        for T in range(16):
            for hf in range(2):
                bk = (2 * T + hf) % 4
                for k in range(8):
                    MM(bank(bk), AT[:, k, T * 128:(T + 1) * 128], WO[:, k, hf * 512:(hf + 1) * 512], k == 0, k == 7,
                       [B_AT[T // 4], B_WO], [PB[bk]])
                dst = R[:, T, hf * 512:(hf + 1) * 512]
                TT("dve", dst, bank(bk), dst, ALU.add, [PB[bk], B_R[T]], [B_R[T]])
        P.barrier()
        AR.release("WO", "AT")

        hT2 = AR.alloc("hT2", [128, 8, NOWN], BF16)
        B_hT2 = [Buf("hT2_%d" % i) for i in range(8)]
        P.dma("sp", gbc, g_mlp.partition_broadcast(128), B_gbc, True)
        W1r = Ring("W1q", 2, [128, 8, 1024], BF16)
        W2r = Ring("W2q", 2, [128, 8, 1024], BF16)
        w_f1_v = w_f1.rearrange("(k p) n -> p k n", p=128)
        w_f2_v = w_f2.rearrange("(k p) n -> p k n", p=128)
        wq = []
        for q_ in range(2):
            w1, Bw1 = W1r.next()
            w2, Bw2 = W2r.next()
            LOADW(w1, w_f1_v[:, :, q_ * 1024:(q_ + 1) * 1024], Bw1)
            LOADW(w2, w_f2_v[:, q_ * 8:(q_ + 1) * 8, :], Bw2)
            wq.append((w1, Bw1, w2, Bw2))
        for T in range(16):
            norm_to_hT(R[:, T, :], B_R[T], hbring, T % 2, hT2[:, :, T * 128:(T + 1) * 128], B_hT2[T // 2], "act")
        uTr = Ring("uT", 3, [128, 8, 256], BF16)
        rlr = Ring("rl", 3, [128, 256], F32)
        ucnt = [0]

        def u_part(w1, Bw1, tg):
            uT, BuT = uTr.next()
            for j in range(8):
                bk = ucnt[0] % 4
                ucnt[0] += 1
                for k in range(8):
                    MM(bank(bk, 256), w1[:, k, j * 128:(j + 1) * 128], hT2[:, k, tg * 256:(tg + 1) * 256],
                       k == 0, k == 7, [Bw1, B_hT2[tg]], [PB[bk]])
                rl, Brl = rlr.next()
                ACTV(rl, bank(bk, 256), AF.Relu, [PB[bk]], [Brl])
                TT("pool" if j % 2 else "dve", uT[:, j, :], rl, rl, ALU.mult, [Brl], [BuT])
            return uT, BuT

        def y_part(w2, Bw2, tg, uT, BuT):
            for t in range(2):
                T = tg * 2 + t
                for hf in range(2):
                    bk = 4 + t * 2 + hf
                    for j in range(8):
                        MM(bank(bk), uT[:, j, t * 128:(t + 1) * 128], w2[:, j, hf * 512:(hf + 1) * 512],
                           j == 0, j == 7, [BuT, Bw2], [PB[bk]])
                    dst = R[:, T, hf * 512:(hf + 1) * 512]
                    TT("dve", dst, bank(bk), dst, ALU.add, [PB[bk], B_R[T]], [B_R[T]])

        work = [(q_, tg) for q_ in range(4) for tg in range(8)]
        prev = None
        for (q_, tg) in work:
            if q_ >= 2 and tg == 0:
                w1, Bw1 = W1r.next()
                w2, Bw2 = W2r.next()
                LOADW(w1, w_f1_v[:, :, q_ * 1024:(q_ + 1) * 1024], Bw1)
                LOADW(w2, w_f2_v[:, q_ * 8:(q_ + 1) * 8, :], Bw2)
                wq.append((w1, Bw1, w2, Bw2))
            w1, Bw1, w2, Bw2 = wq[q_]
            cur = (w2, Bw2, tg) + u_part(w1, Bw1, tg)
            if prev is not None:
                y_part(*prev)
            prev = cur
        y_part(*prev)
        P.barrier()
        W1r.release()
        W2r.release()
        uTr.release()
        rlr.release()

        P.dma("sp", gbc, g_ple.partition_broadcast(128), B_gbc, True)
        Wpg = AR.alloc("Wpg", [128, 8, D], BF16)
        Wp = AR.alloc("Wp", [128, 2, D], BF16)
        B_Wpg, B_Wp = Buf("Wpg"), Buf("Wp")
        LOADW(Wpg, w_pg.rearrange("(k p) n -> p k n", p=128), B_Wpg)
        LOADW(Wp, w_p.rearrange("(k p) n -> p k n", p=128), B_Wp)
        pT = AR.alloc("pT", [128, 2, NOWN], BF16)
        B_pT = [Buf("pT%d" % i) for i in range(16)]
        ppr = Ring("ppf", 2, [128, 256], F32)
        pbr = Ring("ppb", 2, [128, 256], BF16)
        for T in range(16):
            norm_to_hT(R[:, T, :], B_R[T], hbring, T % 2, hT2[:, :, T * 128:(T + 1) * 128], B_hT2[T // 2], "act")
            pf, Bpf = ppr.next()
            P.dma("sp", pf, pp[T * 128:(T + 1) * 128, :], Bpf, True)
            pb, Bpb = pbr.next()
            CP("pool", pb, pf, [Bpf], [Bpb])
            ptp = bankb(2 + T % 2)
            for k in range(2):
                TR(ptp[:, k * 128:(k + 1) * 128], pb[:, k * 128:(k + 1) * 128], [Bpb], [PB[2 + T % 2]])
            CP("dve", pT[:, :, T * 128:(T + 1) * 128], ptp[:, 0:256].rearrange("p (k n) -> p k n", k=2),
               [PB[2 + T % 2]], [B_pT[T]])
        sgr = Ring("sg", 2, [128, 512], F32)
        mpr = Ring("mp", 2, [128, 512], F32)
        for T in range(16):
            for hf in range(2):
                a = (2 * T + hf) % 2
                bg_, be_ = 4 + a, 6 + a
                cs = slice(hf * 512, (hf + 1) * 512)
                for k in range(8):
                    MM(bank(bg_), hT2[:, k, T * 128:(T + 1) * 128], Wpg[:, k, cs], k == 0, k == 7,
                       [B_hT2[T // 2], B_Wpg], [PB[bg_]])
                for k in range(2):
                    MM(bank(be_), pT[:, k, T * 128:(T + 1) * 128], Wp[:, k, cs], k == 0, k == 1,
                       [B_pT[T], B_Wp], [PB[be_]])
                sg, Bsg = sgr.next()
                ACTV(sg, bank(bg_), AF.Sigmoid, [PB[bg_]], [Bsg])
                mp, Bmp = mpr.next()
                TT("dve", mp, bank(be_), sg, ALU.mult, [PB[be_], Bsg], [Bmp])
                dst = R[:, T, cs]
                TT("pool", dst, dst, mp, ALU.add, [Bmp, B_R[T]], [B_R[T]])
        gfin = AR.alloc("gfin", [128, D], F32)
        B_gfin = Buf("gfin")
        P.dma("sp", gfin, g_fin.partition_broadcast(128), B_gfin, True)
        for T in range(16):
            rs, Brs = rms_rstd(R[:, T, :], B_R[T], D)
            STT("dve", R[:, T, :], R[:, T, :], rs[:, 0:1], gfin, ALU.mult, ALU.mult, [B_R[T], Brs, B_gfin], [B_R[T]])
            P.dma("sp", out[T * 128:(T + 1) * 128, :], R[:, T, :], B_R[T], False)
        P.barrier()
        P.emit(st)
    return nc


def _t5_bucket(rel):
    nb = 16
    ret = (rel > 0).astype(np.int32) * nb
    n = np.abs(rel)
    max_exact = nb // 2
    large = max_exact + (np.log(np.maximum(n, 1) / max_exact) / math.log(1024 / max_exact)
                         * (nb - max_exact)).astype(np.int32)
    large = np.minimum(large, nb - 1)
    return ret + np.where(n < max_exact, n, large).astype(np.int32)


def _rope_table():
    half = 64
    inv_freq = np.power(np.float32(10000.0), -np.arange(0, half, 2, dtype=np.float32) / np.float32(half)).astype(np.float32)
    pos = np.arange(S)
    row = (pos // 64).astype(np.float32)
    col = (pos % 64).astype(np.float32)
    ar = (row[:, None] * inv_freq[None, :]).astype(np.float32)
    ac = (col[:, None] * inv_freq[None, :]).astype(np.float32)
    cr, sr, cc, sc = np.cos(ar), np.sin(ar), np.cos(ac), np.sin(ac)
    C = np.concatenate([cr, cr, cc, cc], axis=1)
    Sg = np.concatenate([-sr, sr, -sc, sc], axis=1)
    return np.concatenate([C, Sg], axis=1).astype(np.float32)


def _bias_tables(rel_bias):
    p = np.arange(128)[:, None]
    q = np.arange(128)[None, :]
    tabs = np.empty((6, 128, 512), np.float32)
    for g, c in enumerate(DIL):
        for kind in range(2):
            off = p - 64 + 128 * kind - q
            bucket = _t5_bucket(off * c)
            band = np.abs(off) <= 64
            for hh in range(4):
                vals = rel_bias[bucket, 4 * g + hh]
                tabs[2 * g + kind][:, hh * 128:(hh + 1) * 128] = np.where(band, vals, np.float32(NEG))
    return tabs


def _kmask(r0):
    km = np.zeros((128, 69), np.float32)
    p = np.arange(128)
    for g, c in enumerate(DIL):
        for r in range(c):
            for j in range(NJ[g]):
                th = 1024 + r - 64 * c + 128 * c * j + c * p
                ab = r0 - 1024 + th
                valid = (ab >= 0) & (ab < S)
                km[:, GOFF[g] + r * NJ[g] + j] = np.where(valid, 0.0, NEG)
    return km


def make_in_maps(inputs):
    f = lambda a: np.ascontiguousarray(np.asarray(a, dtype=np.float32))
    x = f(inputs["x"])
    p = f(inputs["p"])
    rope = _rope_table()
    biasT = _bias_tables(f(inputs["rel_bias"]))
    shared = {
        "biasT": biasT,
        "w_in": f(inputs["w_in"][0]), "w_out_a": f(inputs["w_out_a"][0]), "w_out_b": f(inputs["w_out_b"][0]),
        "w_out": f(inputs["w_out"][0]), "w_ff1": f(inputs["w_ff1"][0]), "w_ff2": f(inputs["w_ff2"][0]),
        "w_ple_gate": f(inputs["w_ple_gate"][0]), "w_ple": f(inputs["w_ple"][0]),
        "g_mix": f(inputs["norm_mix_g"][0]).reshape(1, D), "g_mlp": f(inputs["norm_mlp_g"][0]).reshape(1, D),
        "g_ple": f(inputs["norm_ple_g"][0]).reshape(1, D), "g_fin": f(inputs["final_norm_g"]).reshape(1, D),
        "g_q": f(inputs["q_norm_g"][0]).reshape(1, 128), "g_k": f(inputs["k_norm_g"][0]).reshape(1, 128),
        "bgate": f(f(inputs["b_gate"][0]).reshape(16, 128).T),
    }
    maps = []
    for core in range(8):
        b, r0 = core // 4, (core % 4) * NOWN
        xsr = np.ascontiguousarray(np.roll(x[b], -r0, axis=0))
        xhh = np.zeros((4096, D), np.float32)
        lo, hi = r0 - 1024, r0 + 3072
        a0, a1 = max(lo, 0), min(hi, S)
        xhh[a0 - lo:a1 - lo] = x[b, a0:a1]
        m = dict(shared)
        m.update({
            "xs": xsr, "xh": xhh, "pp": np.ascontiguousarray(p[0, b, r0:r0 + NOWN]),
            "rope": np.ascontiguousarray(np.roll(rope, -r0, axis=0)), "kmask": _kmask(r0),
        })
        maps.append(m)
    return maps


_NC_CACHE = {}


def kernel(**inputs):
    if "full" not in _NC_CACHE:
        _NC_CACHE["full"] = build("full")
    nc = _NC_CACHE["full"]
    maps = make_in_maps(inputs)
    res = run_bass_kernel_spmd(nc, maps, core_ids=list(range(8)))
    o = np.empty((2, S, D), np.float32)
    for core in range(8):
        b, r0 = core // 4, (core % 4) * NOWN
        o[b, r0:r0 + NOWN] = res.results[core]["out"]
    return o
```

```python
import math
from contextlib import ExitStack
import numpy as np
import concourse.bass as bass
import concourse.mybir as mybir
from concourse.bass_utils import run_bass_kernel_spmd

F32 = mybir.dt.float32
BF16 = mybir.dt.bfloat16
AF = mybir.ActivationFunctionType
ALU = mybir.AluOpType

ENGS = ("pe", "act", "dve", "pool", "sp")
S = 8192
D = 1024
NOWN = 2048
NEG = -100.0
EPS = 1e-6
DIL = (1, 4, 16)
NJ = (17, 5, 2)
GOFF = (0, 17, 37)


class Buf:
    __slots__ = ("name", "w", "rd", "dsem", "dcnt")

    def __init__(self, name):
        self.name = name
        self.w = None
        self.rd = {}
        self.dsem = None
        self.dcnt = 0


class Ins:
    __slots__ = ("eng", "fn", "waits", "flag", "dbuf", "seq")

    def __init__(self, eng, fn, seq):
        self.eng = eng
        self.fn = fn
        self.waits = []
        self.flag = False
        self.dbuf = None
        self.seq = seq


class Prog:
    def __init__(self, nc):
        self.nc = nc
        self.q = {e: [] for e in ENGS}
        self.waited = {e: {} for e in ENGS}
        self.dma_bufs = []

    def _need(self, ins, tok):
        if tok is None:
            return
        if tok[0] == "e":
            _, eng, seq = tok
            if eng == "pe" and ins.eng == "pe":
                return
            key = ("e", eng)
            val = seq
        else:
            _, buf, cnt = tok
            key = ("d", id(buf))
            val = cnt
        w = self.waited[ins.eng]
        if w.get(key, -1) >= val:
            return
        w[key] = val
        ins.waits.append(tok)
        if tok[0] == "e":
            self.q[tok[1]][tok[2]].flag = True

    def _deps(self, ins, reads, writes):
        for b in reads:
            self._need(ins, b.w)
        for b in writes:
            self._need(ins, b.w)
            for eng, seq in b.rd.items():
                self._need(ins, ("e", eng, seq))
            if b.dcnt:
                self._need(ins, ("d", b, b.dcnt))

    def op(self, eng, fn, reads=(), writes=()):
        ins = Ins(eng, fn, len(self.q[eng]))
        self._deps(ins, reads, writes)
        self.q[eng].append(ins)
        tok = ("e", eng, ins.seq)
        for b in reads:
            b.rd[eng] = ins.seq
        for b in writes:
            b.w = tok
            b.rd = {}
        return ins

    def dma(self, queue, out, in_, track, is_write, reads=(), writes=()):
        ins = Ins(queue, (lambda e: e.dma_start(out=out, in_=in_)), len(self.q[queue]))
        rs = list(reads)
        ws = list(writes)
        (ws if is_write else rs).append(track)
        self._deps(ins, rs, ws)
        if track.dsem is None:
            track.dsem = True
            self.dma_bufs.append(track)
        track.dcnt += 1
        ins.dbuf = track
        self.q[queue].append(ins)
        if is_write:
            track.w = ("d", track, track.dcnt)
            track.rd = {}
        return ins

    def barrier(self):
        last = {}
        for e in ("pe", "act", "dve", "pool"):
            s = len(self.q[e]) - 1
            while s >= 0 and (self.q[e][s].dbuf is not None or self.q[e][s].fn is None):
                s -= 1
            last[e] = s
        dl = [(b, b.dcnt) for b in self.dma_bufs if b.dcnt]
        for e in ENGS:
            ins = Ins(e, None, len(self.q[e]))
            for pe_, s in last.items():
                if s >= 0:
                    self._need(ins, ("e", pe_, s))
            for b, c in dl:
                self._need(ins, ("d", b, c))
            self.q[e].append(ins)

    def emit(self, stack):
        nc = self.nc
        esem = {e: stack.enter_context(nc.semaphore("ms_" + e)) for e in ("pe", "act", "dve", "pool")}
        for i, b in enumerate(self.dma_bufs):
            b.dsem = stack.enter_context(nc.semaphore("d%d_%s" % (i, b.name)))
        rank = {}
        for e in ("pe", "act", "dve", "pool"):
            r = 0
            for ins in self.q[e]:
                if ins.flag:
                    r += 1
                    rank[(e, ins.seq)] = r
        q = self.q

        def run(e, h):
            for ins in q[e]:
                for tok in ins.waits:
                    if tok[0] == "e":
                        h.wait_ge(esem[tok[1]], rank[(tok[1], tok[2])])
                    else:
                        h.wait_ge(tok[1].dsem, 16 * tok[2])
                if ins.fn is None:
                    continue
                bi = ins.fn(h)
                if ins.dbuf is not None:
                    bi.then_inc(ins.dbuf.dsem, 16)
                elif ins.flag:
                    bi.then_inc(esem[e], 1)

        block = stack.enter_context(nc.Block())

        @block.tensor
        def _(t):
            run("pe", t)

        @block.scalar
        def _(t):
            run("act", t)

        @block.vector
        def _(t):
            run("dve", t)

        @block.gpsimd
        def _(t):
            run("pool", t)

        @block.sync
        def _(t):
            run("sp", t)


class Arena:
    def __init__(self, nc, nwords):
        self.ap = nc.alloc_sbuf_tensor("arena", [128, nwords], F32).ap()
        self.free = [(0, nwords)]
        self.live = {}

    def alloc(self, name, shape, dt, top=False):
        n = int(np.prod(shape[1:]))
        nbytes = n * (4 if dt == F32 else 2)
        w = (nbytes + 31) // 32 * 8
        order = range(len(self.free) - 1, -1, -1) if top else range(len(self.free))
        for i in order:
            o, sz = self.free[i]
            if sz >= w:
                if top:
                    off = o + sz - w
                    if sz == w:
                        self.free.pop(i)
                    else:
                        self.free[i] = (o, sz - w)
                else:
                    off = o
                    if sz == w:
                        self.free.pop(i)
                    else:
                        self.free[i] = (o + w, sz - w)
                break
        else:
            raise RuntimeError("arena full allocating %s (%d words); free=%s" % (name, w, self.free))
        assert name not in self.live, name
        self.live[name] = (off, w)
        a = self.ap[0:shape[0], off:off + w]
        if dt != F32:
            a = a.bitcast(dt)
        a = a[:, 0:n]
        if len(shape) == 3:
            a = a.rearrange("p (a b) -> p a b", a=shape[1])
        elif len(shape) == 4:
            a = a.rearrange("p (a b c) -> p a b c", a=shape[1], b=shape[2])
        return a

    def release(self, *names):
        for name in names:
            off, w = self.live.pop(name)
            self.free.append((off, w))
        self.free.sort()
        m = []
        for o, sz in self.free:
            if m and m[-1][0] + m[-1][1] == o:
                m[-1] = (m[-1][0], m[-1][1] + sz)
            else:
                m.append((o, sz))
        self.free = m


def build(stage="full"):
    nc = bass.Bass("TRN2", target_bir_lowering=False)

    def din(name, shape):
        return nc.dram_tensor(name, list(shape), F32, kind="ExternalInput").ap()

    xs = din("xs", [S, D])
    xh = din("xh", [4096, D])
    pp = din("pp", [NOWN, 256])
    rope = din("rope", [S, 256])
    kmask_d = din("kmask", [128, 69])
    biasT_d = din("biasT", [6, 128, 512])
    w_in = din("w_in", [D, 5888])
    w_oa = din("w_out_a", [D, D])
    w_ob = din("w_out_b", [256, D])
    w_o = din("w_out", [D, D])
    w_f1 = din("w_ff1", [D, 4096])
    w_f2 = din("w_ff2", [4096, D])
    w_pg = din("w_ple_gate", [D, D])
    w_p = din("w_ple", [256, D])
    g_mix = din("g_mix", [1, D])
    g_mlp = din("g_mlp", [1, D])
    g_ple = din("g_ple", [1, D])
    g_fin = din("g_fin", [1, D])
    g_q = din("g_q", [1, 128])
    g_k = din("g_k", [1, 128])
    bgate_d = din("bgate", [128, 16])
    out = nc.dram_tensor("out", [NOWN, D], F32, kind="ExternalOutput").ap()
    vscr = nc.dram_tensor("vscr", [4096, 768], BF16).ap()
    dbg = None
    if stage == "B":
        dbg = nc.dram_tensor("dbg", [128, 2 * NOWN], F32, kind="ExternalOutput").ap()
    elif stage == "A":
        dbg = nc.dram_tensor("dbg", [128, 8 * NOWN], F32, kind="ExternalOutput").ap()

    w_in_v = w_in.rearrange("(k p) n -> p k n", p=128)

    st = ExitStack()
    with st:
        P = Prog(nc)
        AR = Arena(nc, 50400)
        ps = st.enter_context(nc.psum_tensor("ps", [128, 4096], F32)).ap()
        PB = [Buf("ps%d" % i) for i in range(8)]

        def bank(i, n=512):
            return ps[:, i * 512:i * 512 + n]

        def bankb(i, n=1024):
            return ps[:, i * 512:(i + 1) * 512].bitcast(BF16)[:, 0:n]

        def MM(o, lhsT, rhs, start, stop, reads, writes):
            P.op("pe", lambda e: e.matmul(o, lhsT, rhs, start=start, stop=stop), reads, writes)

        def TR(o, i, reads, writes):
            P.op("pe", lambda e: e.transpose(o, i, ident), list(reads) + [B_ident], writes)

        def ACTV(o, i, func, reads, writes, bias=0.0, scale=1.0, accum=None):
            P.op("act", lambda e: e.activation(o, i, func, bias=bias, scale=scale, accum_out=accum), reads, writes)

        def TT(eng, o, a, b, op, reads, writes):
            P.op(eng, lambda e: e.tensor_tensor(o, a, b, op), reads, writes)

        def STT(eng, o, a, sc, b, op0, op1, reads, writes):
            P.op(eng, lambda e: e.scalar_tensor_tensor(o, a, sc, b, op0, op1), reads, writes)

        def CP(eng, o, i, reads, writes):
            if eng == "act":
                P.op("act", lambda e: e.copy(o, i), reads, writes)
            else:
                P.op(eng, lambda e: e.tensor_copy(o, i), reads, writes)

        def RECIP(o, i, reads, writes):
            P.op("dve", lambda e: e.reciprocal(o, i), reads, writes)

        def MEMSET(eng, o, v, writes):
            P.op(eng, lambda e: e.memset(o, v), (), writes)

        def LOADW(dst, src, buf):
            for k in range(dst.shape[1]):
                P.dma("pool", dst[:, k, :], src[:, k, :], buf, True)

        class Ring:
            def __init__(self, name, n, shape, dt, top=False):
                self.aps = [AR.alloc("%s%d" % (name, i), shape, dt, top=top) for i in range(n)]
                self.bufs = [Buf("%s%d" % (name, i)) for i in range(n)]
                self.names = ["%s%d" % (name, i) for i in range(n)]
                self.i = 0

            def next(self):
                k = self.i % len(self.aps)
                self.i += 1
                return self.aps[k], self.bufs[k]

            def release(self):
                AR.release(*self.names)

        ident = AR.alloc("ident", [128, 128], BF16)
        B_ident = Buf("ident")
        identf = AR.alloc("identf", [128, 128], F32)
        B_identf = Buf("identf")
        MEMSET("pool", identf, 0.0, [B_identf])
        P.op("pool", lambda e: e.affine_select(identf, identf, [[-1, 128]], ALU.not_equal, 1.0, base=0,
                                               channel_multiplier=1), [B_identf], [B_identf])
        CP("dve", ident, identf, [B_identf], [B_ident])
        ones = AR.alloc("ones", [128, 128], BF16)
        B_ones = Buf("ones")
        MEMSET("pool", ones, 1.0, [B_ones])
        gq_bc = AR.alloc("gq_bc", [128, 128], F32)
        gk_bc = AR.alloc("gk_bc", [128, 128], F32)
        B_gq = Buf("gq")
        B_gk = Buf("gk")
        P.dma("sp", gq_bc, g_q.partition_broadcast(128), B_gq, True)
        P.dma("sp", gk_bc, g_k.partition_broadcast(128), B_gk, True)
        bgate = AR.alloc("bgate", [128, 16], F32)
        B_bgate = Buf("bgate")
        P.dma("sp", bgate, bgate_d, B_bgate, True)
        gbc = AR.alloc("gbc", [128, D], F32)
        B_gbc = Buf("gbc")
        P.dma("sp", gbc, g_mix.partition_broadcast(128), B_gbc, True)
        stat = AR.alloc("stat", [128, 128], F32)
        sqj = AR.alloc("sqj", [128, D], F32)
        B_sqj = Buf("sqj")
        statbufs = [Buf("stat%d" % i) for i in range(32)]
        stat_i = [0]

        def stat_slot(n=1):
            assert n <= 4
            k = stat_i[0] % 32
            stat_i[0] += 1
            return stat[:, 4 * k:4 * k + n], statbufs[k]

        def rms_rstd(x_ap, Bx, width, ncols=1, xs_list=None):
            ss, Bss = stat_slot(ncols)
            srcs = xs_list if xs_list is not None else [x_ap]
            for c, src in enumerate(srcs):
                ACTV(sqj[:, 0:width], src, AF.Square, [Bx], [B_sqj, Bss], accum=ss[:, c:c + 1])
            rs, Brs = stat_slot(ncols)
            ACTV(rs, ss, AF.Sqrt, [Bss], [Brs], bias=EPS, scale=1.0 / width)
            RECIP(rs, rs, [Brs], [Brs])
            return rs, Brs

        def norm_to_hT(x_ap, Bx, hb_ring, trbank, dst_ap, Bdst, copy_eng):
            rs, Brs = rms_rstd(x_ap, Bx, D)
            hb, Bhb = hb_ring.next()
            STT("dve", hb, x_ap, rs[:, 0:1], gbc, ALU.mult, ALU.mult, [Bx, Brs, B_gbc], [Bhb])
            pt = bankb(trbank)
            for k in range(8):
                TR(pt[:, k * 128:(k + 1) * 128], hb[:, k * 128:(k + 1) * 128], [Bhb], [PB[trbank]])
            CP(copy_eng, dst_ap, pt.rearrange("p (k n) -> p k n", k=8), [PB[trbank]], [Bdst])

        KBT = AR.alloc("KBT", [128, 6, 4096], BF16)
        QBT = AR.alloc("QBT", [128, 6, NOWN], BF16)
        B_KBT = Buf("KBT")
        B_QBT = Buf("QBT")
        WB = AR.alloc("WB", [128, 8, 2304], BF16)
        B_WB = Buf("WB")
        LOADW(WB, w_in_v[:, :, 1536:3840], B_WB)
        xring = Ring("xr", 3, [128, D], F32)
        hbring = Ring("hb", 2, [128, D], BF16)
        hTg = Ring("hTg", 2, [128, 8, 512], BF16)
        vtr = Ring("vt", 2, [128, 768], BF16)
        for G in range(0 if stage[0] == "x" else (8 if stage != "1a1" else 1)):
            hT, BhT = hTg.next()
            for t in range(4):
                T = G * 4 + t
                xa, Bx = xring.next()
                P.dma("sp", xa, xh[T * 128:(T + 1) * 128, :], Bx, True)
                norm_to_hT(xa, Bx, hbring, T % 2, hT[:, :, t * 128:(t + 1) * 128], BhT, "act")
            for c in range(6):
                bk = 2 + (c % 2)
                for k in range(8):
                    MM(bank(bk), WB[:, k, 768 + c * 128:768 + (c + 1) * 128], hT[:, k, :], k == 0, k == 7,
                       [B_WB, BhT], [PB[bk]])
                CP("dve", KBT[:, c, G * 512:(G + 1) * 512], bank(bk), [PB[bk]], [B_KBT])
            if 2 <= G < 6:
                for c in range(6):
                    bk = 2 + (c % 2)
                    for k in range(8):
                        MM(bank(bk), WB[:, k, c * 128:(c + 1) * 128], hT[:, k, :], k == 0, k == 7,
                           [B_WB, BhT], [PB[bk]])
                    ACTV(QBT[:, c, (G - 2) * 512:(G - 1) * 512], bank(bk), AF.Copy, [PB[bk]], [B_QBT], scale=0.125)
            for t in range(4):
                T = G * 4 + t
                b0 = 4 + 2 * (t % 2)
                for k in range(8):
                    MM(bank(b0), hT[:, k, t * 128:(t + 1) * 128], WB[:, k, 1536:2048], k == 0, k == 7,
                       [B_WB, BhT], [PB[b0]])
                for k in range(8):
                    MM(bank(b0 + 1, 256), hT[:, k, t * 128:(t + 1) * 128], WB[:, k, 2048:2304], k == 0, k == 7,
                       [B_WB, BhT], [PB[b0 + 1]])
                vt, Bvt = vtr.next()
                CP("dve", vt[:, 0:512], bank(b0), [PB[b0]], [Bvt])
                CP("act", vt[:, 512:768], bank(b0 + 1, 256), [PB[b0 + 1]], [Bvt])
                P.dma("sp", vscr[T * 128:(T + 1) * 128, :], vt, Bvt, False)
        P.barrier()
        if stage in ("1a", "1a1"):
            P.emit(st)
            return nc
        AR.release("WB")
        xring.release()
        hbring.release()
        hTg.release()
        vtr.release()

        BT = AR.alloc("BT", [128, 2, NOWN], BF16, top=True)
        B_BT = Buf("BT")
        Vext = AR.alloc("Vext", [128, 32, 4, 128], BF16)
        B_Vext = Buf("Vext")
        Vraw = AR.alloc("Vraw", [128, 32, 256], BF16)
        B_Vraw = Buf("Vraw")
        Bacc = AR.alloc("Bacc", [128, 4, NOWN], F32)
        B_Bacc = Buf("Bacc")
        kmask = AR.alloc("kmask", [128, 69], F32)
        B_kmask = Buf("kmask")
        P.dma("sp", kmask, kmask_d, B_kmask, True)
        biasr = Ring("bias", 2, [128, 2, 512], F32)
        s2r = Ring("s2", 2, [128, 512], F32)
        ptr = Ring("ptb", 3, [128, 512], BF16)
        Vext3 = Vext.rearrange("p j h n -> p (j h) n")
        MEMSET("pool", Vext3[:, :, 64:128], 1.0, [B_Vext])
        qpr = Ring("qpad", 3, [128, 2, 256], BF16)
        for qa_, qb_ in zip(qpr.aps, qpr.bufs):
            MEMSET("pool", qa_, 0.0, [qb_])
        nblk = 0
        for g in range(3):
            c = DIL[g]
            nj = NJ[g]
            Lq = NOWN // c
            bias_ap, Bbias = biasr.next()
            P.dma("sp", bias_ap, biasT_d[2 * g:2 * g + 2].rearrange("k p n -> p k n"), Bbias, True)
            for r in range(c):
                base = 1024 + r - 64 * c
                for j0 in range(0, nj, 4):
                    j1 = min(nj, j0 + 4)
                    b0_ = base + c * 128 * j0
                    src = vscr[b0_:b0_ + c * (128 * (j1 - j0) - 1) + 1:c, g * 256:(g + 1) * 256].rearrange(
                        "(j p) n -> p j n", p=128)
                    P.dma("sp", Vraw[:, r * nj + j0:r * nj + j1, :], src, B_Vraw, True)
            CP("pool", Vext3[:, 0:c * nj * 4, 0:64],
               Vraw[:, 0:c * nj, :].rearrange("p j (h d) -> p (j h) d", h=4), [B_Vraw], [B_Vext])
            xl = int(stage[1]) if stage[0] == "x" else 9
            for r in range(c):
                if stage == "1b_a" or (stage in ("1b_b",) and g > 0) or (stage[0] == "x" and g > 0):
                    break
                for i in range(Lq // 128 if stage[0] != "x" else 2):
                    sb = [2 * (nblk % 2), 2 * (nblk % 2) + 1]
                    ob = 4 + (nblk % 2)
                    nblk += 1
                    pts = []
                    qstart = r + c * 128 * i
                    qp, Bqp = qpr.next()
                    for cc in range(2):
                        ch = 2 * g + cc
                        CP("act", qp[0:64, cc, 0:128], QBT[0:64, ch, qstart:qstart + c * 127 + 1:c], [B_QBT], [Bqp])
                        CP("act", qp[64:128, cc, 128:256], QBT[64:128, ch, qstart:qstart + c * 127 + 1:c], [B_QBT], [Bqp])
                    for kind in range(2):
                        j = i + kind
                        kstart = 1024 + r + c * (128 * j - 64)
                        for cc in range(2):
                            ch = 2 * g + cc
                            MM(bank(sb[kind])[:, cc * 256:(cc + 1) * 256],
                               KBT[:, ch, kstart:kstart + c * 127 + 1:c], qp[:, cc, :], True, True,
                               [B_KBT, Bqp], [PB[sb[kind]]])
                        if xl < 2:
                            continue
                        s2, Bs2 = s2r.next()
                        TT("dve", s2, bank(sb[kind]), bias_ap[:, kind, :], ALU.add, [PB[sb[kind]], Bbias], [Bs2])
                        pt, Bpt = ptr.next()
                        tile_idx = GOFF[g] + r * nj + j
                        ACTV(pt, s2, AF.Exp, [Bs2, B_kmask], [Bpt], bias=kmask[:, tile_idx:tile_idx + 1])
                        pts.append((pt, Bpt))
                    if xl < 3:
                        continue
                    for hh in range(4):
                        for kind in range(2):
                            j = i + kind
                            slot = r * nj + j
                            lhsT = Vext[:, slot, hh, :]
                            MM(bank(ob)[:, hh * 128:(hh + 1) * 128], lhsT, pts[kind][0][:, hh * 128:(hh + 1) * 128],
                               kind == 0, kind == 1, [B_Vext, pts[kind][1]], [PB[ob]])
                    if xl < 4:
                        continue
                    qstart = r + c * 128 * i
                    dst = Bacc[:, :, qstart:qstart + c * 127 + 1:c]
                    srcp = bank(ob).rearrange("p (h n) -> p h n", h=4)
                    if g == 0:
                        CP("dve", dst, srcp, [PB[ob]], [B_Bacc])
                    else:
                        TT("dve", dst, srcp, dst, ALU.add, [PB[ob], B_Bacc], [B_Bacc])
        P.barrier()
        if stage in ("1b_a", "1b_b", "B2") or stage[0] == "x":
            P.emit(st)
            return nc
        AR.release("Vext", "Vraw")
        denlo = AR.alloc("denlo", [64, 4, NOWN], F32)
        B_denlo = Buf("denlo")
        for hh in range(4):
            P.dma("sp", denlo[:, hh, :], Bacc[64:128, hh, :], B_denlo, True, reads=[B_Bacc])
        BTo = AR.alloc("BTo", [64, 2, NOWN], BF16)
        B_BTo = Buf("BTo")
        for hh in range(4):
            RECIP(denlo[:, hh, :], denlo[:, hh, :], [B_denlo], [B_denlo])
            if hh % 2 == 0:
                TT("dve", BT[0:64, hh // 2, :], Bacc[0:64, hh, :], denlo[:, hh, :], ALU.mult, [B_Bacc, B_denlo], [B_BT])
            else:
                TT("dve", BTo[:, hh // 2, :], Bacc[0:64, hh, :], denlo[:, hh, :], ALU.mult, [B_Bacc, B_denlo], [B_BTo])
        for k2 in range(2):
            P.dma("sp", BT[64:128, k2, :], BTo[:, k2, :], B_BT, True, reads=[B_BTo])
        if stage == "B":
            P.barrier()
            AR.release("Bacc")
            dtmp = AR.alloc("dtmpB", [128, 2 * NOWN], F32)
            B_dtmp = Buf("dtmpB")
            CP("dve", dtmp, BT.rearrange("p a n -> p (a n)"), [B_BT], [B_dtmp])
            P.dma("sp", dbg[:, 0:2048], dtmp[:, 0:2048], B_dtmp, False)
            P.dma("sp", dbg[:, 2048:4096], dtmp[:, 2048:4096], B_dtmp, False)
            P.barrier()
            P.emit(st)
            return nc
        P.barrier()
        AR.release("denlo", "Bacc", "kmask", "KBT", "QBT", "BTo")
        qpr.release()
        biasr.release()
        s2r.release()
        ptr.release()

        KAT = AR.alloc("KAT", [128, 2, S], BF16)
        VA = AR.alloc("VA", [128, 64, 256], BF16)
        QAT = AR.alloc("QAT", [128, 8, NOWN], BF16)
        B_KAT = [Buf("KAT%d" % i) for i in range(16)]
        B_VA = [Buf("VA%d" % i) for i in range(16)]
        B_QAT = [Buf("QAT%d" % i) for i in range(4)]
        WA = AR.alloc("WA", [128, 8, 1536], BF16)
        B_WA = Buf("WA")
        LOADW(WA, w_in_v[:, :, 0:1536], B_WA)
        xring = Ring("xr", 3, [128, D], F32)
        rpr = Ring("rp", 3, [128, 256], F32)
        hbring = Ring("hb", 2, [128, D], BF16)
        hTr = Ring("hTt", 3, [128, 8, 128], BF16)
        knr = Ring("kn", 2, [128, 512], F32)
        t1r = Ring("t1", 2, [128, 512], F32)
        t2r = Ring("t2", 2, [128, 512], F32)
        krr = Ring("kr", 2, [128, 512], BF16)

        def qk_post(src_bank, Bsrc, nh, g_bc, Bg, rp, Brp, dst_fn):
            W = nh * 128
            rs, Brs = rms_rstd(None, Bsrc, 128, ncols=nh, xs_list=[src_bank[:, h * 128:(h + 1) * 128] for h in range(nh)])
            kn, Bkn = knr.next()
            for h in range(nh):
                STT("dve", kn[:, h * 128:(h + 1) * 128], src_bank[:, h * 128:(h + 1) * 128], rs[:, h:h + 1], g_bc,
                    ALU.mult, ALU.mult, [Bsrc, Brs, Bg], [Bkn])
            t1, Bt1 = t1r.next()
            t2, Bt2 = t2r.next()
            knv = kn[:, 0:W].rearrange("p (h n) -> p h n", h=nh)
            Cb = rp[:, 0:128].unsqueeze(1).to_broadcast([128, nh, 128])
            TT("pool", t1[:, 0:W].rearrange("p (h n) -> p h n", h=nh), knv, Cb, ALU.mult, [Bkn, Brp], [Bt1])
            kn5 = kn[:, 0:W].rearrange("p (h a b d) -> p h a b d", h=nh, a=2, b=2)
            t25 = t2[:, 0:W].rearrange("p (h a b d) -> p h a b d", h=nh, a=2, b=2)
            S5 = rp[:, 128:256].rearrange("p (a b d) -> p a b d", a=2, b=2)
            for b_ in range(2):
                for a_ in range(2):
                    Sb = S5[:, a_, b_, :].unsqueeze(1).to_broadcast([128, nh, 32])
                    TT("pool", t25[:, :, a_, b_, :], kn5[:, :, a_, 1 - b_, :], Sb, ALU.mult, [Bkn, Brp], [Bt2])
            kr, Bkr = krr.next()
            TT("dve", kr[:, 0:W], t1[:, 0:W], t2[:, 0:W], ALU.add, [Bt1, Bt2], [Bkr])
            return kr, Bkr

        def rms_rstd_g(Bx, width, srcs):
            ncols = len(srcs)
            ss, Bss = stat_slot(ncols)
            for c_, src in enumerate(srcs):
                ACTV(sqj[:, 0:width], src, AF.Square, [Bx], [B_sqj, Bss], accum=ss[:, c_:c_ + 1])
            yield None
            rs, Brs = stat_slot(ncols)
            ACTV(rs, ss, AF.Sqrt, [Bss], [Brs], bias=EPS, scale=1.0 / width)
            yield None
            RECIP(rs, rs, [Brs], [Brs])
            yield None
            yield (rs, Brs)

        def qk_post_g(src_bank, Bsrc, nh, g_bc, Bg, rp, Brp, res):
            W = nh * 128
            rsb = None
            for v in rms_rstd_g(Bsrc, 128, [src_bank[:, h * 128:(h + 1) * 128] for h in range(nh)]):
                if v is None:
                    yield
                else:
                    rsb = v
            rs, Brs = rsb
            kn, Bkn = knr.next()
            for h in range(nh):
                STT("dve", kn[:, h * 128:(h + 1) * 128], src_bank[:, h * 128:(h + 1) * 128], rs[:, h:h + 1], g_bc,
                    ALU.mult, ALU.mult, [Bsrc, Brs, Bg], [Bkn])
            yield
            t1, Bt1 = t1r.next()
            t2, Bt2 = t2r.next()
            knv = kn[:, 0:W].rearrange("p (h n) -> p h n", h=nh)
            Cb = rp[:, 0:128].unsqueeze(1).to_broadcast([128, nh, 128])
            TT("pool", t1[:, 0:W].rearrange("p (h n) -> p h n", h=nh), knv, Cb, ALU.mult, [Bkn, Brp], [Bt1])
            kn5 = kn[:, 0:W].rearrange("p (h a b d) -> p h a b d", h=nh, a=2, b=2)
            t25 = t2[:, 0:W].rearrange("p (h a b d) -> p h a b d", h=nh, a=2, b=2)
            S5 = rp[:, 128:256].rearrange("p (a b d) -> p a b d", a=2, b=2)
            for b_ in range(2):
                for a_ in range(2):
                    Sb = S5[:, a_, b_, :].unsqueeze(1).to_broadcast([128, nh, 32])
                    TT("pool", t25[:, :, a_, b_, :], kn5[:, :, a_, 1 - b_, :], Sb, ALU.mult, [Bkn, Brp], [Bt2])
            yield
            kr, Bkr = krr.next()
            TT("dve", kr[:, 0:W], t1[:, 0:W], t2[:, 0:W], ALU.add, [Bt1, Bt2], [Bkr])
            res.append((kr, Bkr))
            yield

        def tile2_g(T):
            xa, Bx = xring.next()
            P.dma("sp", xa, xs[T * 128:(T + 1) * 128, :], Bx, True)
            rp, Brp = rpr.next()
            P.dma("sp", rp, rope[T * 128:(T + 1) * 128, :], Brp, True)
            hT, BhT = hTr.next()
            yield
            rsb = None
            for v in rms_rstd_g(Bx, D, [xa]):
                if v is None:
                    yield
                else:
                    rsb = v
            rs, Brs = rsb
            hb, Bhb = hbring.next()
            STT("dve", hb, xa, rs[:, 0:1], gbc, ALU.mult, ALU.mult, [Bx, Brs, B_gbc], [Bhb])
            yield
            trb = T % 2
            pt = bankb(trb)
            for k in range(8):
                TR(pt[:, k * 128:(k + 1) * 128], hb[:, k * 128:(k + 1) * 128], [Bhb], [PB[trb]])
            yield
            CP("act", hT, pt.rearrange("p (k n) -> p k n", k=8), [PB[trb]], [BhT])
            yield
            bkv = 2 + (T % 2)
            for k in range(8):
                MM(bank(bkv), hT[:, k, :], WA[:, k, 1024:1536], k == 0, k == 7, [BhT, B_WA], [PB[bkv]])
            own = T < 16
            par = T % 2
            yield
            CP("act", VA[:, T, :], bank(bkv)[:, 256:512], [PB[bkv]], [B_VA[T // 4]])
            res = []
            for _ in qk_post_g(bank(bkv)[:, 0:256], PB[bkv], 2, gk_bc, B_gk, rp, Brp, res):
                yield
            kr, Bkr = res[0]
            ptk = bankb(6)[:, par * 256:par * 256 + 256]
            for h in range(2):
                TR(ptk[:, h * 128:(h + 1) * 128], kr[:, h * 128:(h + 1) * 128], [Bkr], [PB[6]])
            yield
            CP("act", KAT[:, :, T * 128:(T + 1) * 128], ptk.rearrange("p (h n) -> p h n", h=2),
               [PB[6]], [B_KAT[T // 4]])
            if own:
                ptq = bankb(7)[:, par * 512:par * 512 + 512]
                bq = 4 + par
                for hq in range(2):
                    for k in range(8):
                        MM(bank(bq), hT[:, k, :], WA[:, k, hq * 512:(hq + 1) * 512], k == 0, k == 7,
                           [BhT, B_WA], [PB[bq]])
                    yield
                    res = []
                    for _ in qk_post_g(bank(bq), PB[bq], 4, gq_bc, B_gq, rp, Brp, res):
                        yield
                    qr, Bqr = res[0]
                    for h in range(4):
                        TR(ptq[:, h * 128:(h + 1) * 128], qr[:, h * 128:(h + 1) * 128], [Bqr], [PB[7]])
                    yield
                    CP("dve", QAT[:, hq * 4:(hq + 1) * 4, T * 128:(T + 1) * 128], ptq.rearrange("p (h n) -> p h n", h=4),
                       [PB[7]], [B_QAT[T // 4]])
                    yield
            yield

        NLOCK = 2
        pend = list(range(64))
        active = []
        while pend or active:
            while pend and len(active) < NLOCK:
                active.append(tile2_g(pend.pop(0)))
            for g_ in list(active):
                try:
                    next(g_)
                except StopIteration:
                    active.remove(g_)
        P.barrier()
        if stage == "P2":
            P.emit(st)
            return nc
        AR.release("WA")
        for r_ in (xring, rpr, hbring, hTr, knr, t1r, t2r, krr):
            r_.release()

        AT = AR.alloc("AT", [128, 8, NOWN], BF16, top=True)
        B_AT = [Buf("AT%d" % i) for i in range(4)]
        ptr = Ring("ptA", 3, [128, 1024], BF16)
        rdr = Ring("rden", 2, [128, 512], F32)
        scale_a = 128.0 ** -0.5
        for qg in range(4):
            for h in range(8):
                kv = h // 4
                it = qg * 8 + h
                ob = 4 + (it % 2)
                db = 6 + (it % 2)
                q_ap = QAT[:, h, qg * 512:(qg + 1) * 512]

                def qk(jp):
                    sb0 = 2 * (jp % 2)
                    for u in range(2):
                        kt = 2 * jp + u
                        MM(bank(sb0 + u), KAT[:, kv, kt * 128:(kt + 1) * 128], q_ap, True, True,
                           [B_KAT[kt // 4], B_QAT[qg]], [PB[sb0 + u]])

                def ex(jp):
                    sb0 = 2 * (jp % 2)
                    pt, Bpt = ptr.next()
                    ACTV(pt, ps[:, sb0 * 512:(sb0 + 2) * 512], AF.Exp, [PB[sb0], PB[sb0 + 1]], [Bpt], scale=scale_a)
                    return pt, Bpt

                def pv(jp, pt, Bpt):
                    for u in range(2):
                        kt = 2 * jp + u
                        MM(bank(ob), VA[:, kt, kv * 128:(kv + 1) * 128], pt[:, u * 512:(u + 1) * 512],
                           kt == 0, kt == 63, [B_VA[kt // 4], Bpt], [PB[ob]])
                        MM(bank(db), ones, pt[:, u * 512:(u + 1) * 512], kt == 0, kt == 63,
                           [B_ones, Bpt], [PB[db]])

                qk(0)
                for jp in range(32):
                    if jp + 1 < 32:
                        qk(jp + 1)
                    pt, Bpt = ex(jp)
                    pv(jp, pt, Bpt)
                rd, Brd = rdr.next()
                RECIP(rd, bank(db), [PB[db]], [Brd])
                TT("dve", AT[:, h, qg * 512:(qg + 1) * 512], bank(ob), rd, ALU.mult, [PB[ob], Brd], [B_AT[qg]])
        if stage == "A":
            P.barrier()
            AR.release("KAT", "VA")
            dtmp = AR.alloc("dtmp", [128, 8 * NOWN], F32)
            B_dtmp = Buf("dtmp")
            CP("dve", dtmp, AT.rearrange("p a n -> p (a n)"), B_AT, [B_dtmp])
            P.dma("sp", dbg, dtmp, B_dtmp, False)
            P.barrier()
            P.emit(st)
            return nc
        P.barrier()
        AR.release("KAT", "VA", "QAT")
        ptr.release()
        rdr.release()

        Wg = AR.alloc("Wg", [128, 8, 2048], BF16)
        WoA = AR.alloc("WoA", [128, 8, D], BF16)
        WoB = AR.alloc("WoB", [128, 2, D], BF16)
        B_Wg, B_WoA, B_WoB = Buf("Wg"), Buf("WoA"), Buf("WoB")
        LOADW(Wg, w_in_v[:, :, 3840:5888], B_Wg)
        LOADW(WoA, w_oa.rearrange("(k p) n -> p k n", p=128), B_WoA)
        LOADW(WoB, w_ob.rearrange("(k p) n -> p k n", p=128), B_WoB)
        xring = Ring("xr", 2, [128, D], F32)
        hbring = Ring("hb", 2, [128, D], BF16, top=True)
        hTg = Ring("hTg", 2, [128, 8, 512], BF16)
        Gtmp = AR.alloc("Gtmp", [128, 8, 512], BF16)
        B_Gtmp = Buf("Gtmp")
        sar = Ring("sa", 2, [128, 512], F32)
        sbr = Ring("sb", 2, [128, 512], F32)
        m1r = Ring("m1", 2, [128, 512], F32)
        m2r = Ring("m2", 2, [128, 512], F32)
        for grp in range(4):
            hT, BhT = hTg.next()
            for t in range(4):
                T = grp * 4 + t
                xa, Bx = xring.next()
                P.dma("sp", xa, xs[T * 128:(T + 1) * 128, :], Bx, True)
                norm_to_hT(xa, Bx, hbring, 0, hT[:, :, t * 128:(t + 1) * 128], BhT, "act")
            tok = slice(grp * 512, (grp + 1) * 512)
            for c in range(8):
                a = c % 2
                bga, bgb, bya, byb = 1 + a, 7, 3 + a, 5 + a
                cs = slice(c * 128, (c + 1) * 128)
                for k in range(8):
                    MM(bank(bga), Wg[:, k, cs], hT[:, k, :], k == 0, k == 7, [B_Wg, BhT], [PB[bga]])
                sa, Bsa = sar.next()
                ACTV(sa, bank(bga), AF.Sigmoid, [PB[bga], B_bgate], [Bsa], bias=bgate[:, c:c + 1])
                for k in range(8):
                    MM(bank(bgb), Wg[:, k, 1024 + c * 128:1024 + (c + 1) * 128], hT[:, k, :], k == 0, k == 7,
                       [B_Wg, BhT], [PB[bgb]])
                sb_, Bsb = sbr.next()
                ACTV(sb_, bank(bgb), AF.Sigmoid, [PB[bgb], B_bgate], [Bsb], bias=bgate[:, 8 + c:9 + c])
                for k in range(8):
                    MM(bank(bya), WoA[:, k, cs], AT[:, k, tok], k == 0, k == 7, [B_WoA, B_AT[grp]], [PB[bya]])
                for k in range(2):
                    MM(bank(byb), WoB[:, k, cs], BT[:, k, tok], k == 0, k == 1, [B_WoB, B_BT], [PB[byb]])
                m1, Bm1 = m1r.next()
                m2, Bm2 = m2r.next()
                TT("dve", m1, bank(bya), sa, ALU.mult, [PB[bya], Bsa], [Bm1])
                TT("dve", m2, bank(byb), sb_, ALU.mult, [PB[byb], Bsb], [Bm2])
                TT("pool", Gtmp[:, c, :], m1, m2, ALU.add, [Bm1, Bm2], [B_Gtmp])
            CP("pool", AT[:, :, tok], Gtmp, [B_Gtmp], [B_AT[grp]])
        P.barrier()
        AR.release("Wg", "WoA", "WoB", "Gtmp", "BT")
        for r_ in (xring, hTg, sar, sbr, m1r, m2r):
            r_.release()

        R = AR.alloc("R", [128, 16, D], F32)
        B_R = [Buf("R%d" % i) for i in range(16)]
        WO = AR.alloc("WO", [128, 8, D], BF16)
        B_WO = Buf("WO")
        LOADW(WO, w_o.rearrange("(k p) n -> p k n", p=128), B_WO)
        for T in range(16):
            P.dma("sp", R[:, T, :], xs[T * 128:(T + 1) * 128, :], B_R[T], True)
        for T in range(16):
            for hf in range(2):
                bk = (2 * T + hf) % 4
                for k in range(8):
                    MM(bank(bk), AT[:, k, T * 128:(T + 1) * 128], WO[:, k, hf * 512:(hf + 1) * 512], k == 0, k == 7,
                       [B_AT[T // 4], B_WO], [PB[bk]])
                dst = R[:, T, hf * 512:(hf + 1) * 512]
                TT("dve", dst, bank(bk), dst, ALU.add, [PB[bk], B_R[T]], [B_R[T]])
        P.barrier()
        AR.release("WO", "AT")

        hT2 = AR.alloc("hT2", [128, 8, NOWN], BF16)
        B_hT2 = [Buf("hT2_%d" % i) for i in range(8)]
        P.dma("sp", gbc, g_mlp.partition_broadcast(128), B_gbc, True)
        W1r = Ring("W1q", 2, [128, 8, 1024], BF16)
        W2r = Ring("W2q", 2, [128, 8, 1024], BF16)
        w_f1_v = w_f1.rearrange("(k p) n -> p k n", p=128)
        w_f2_v = w_f2.rearrange("(k p) n -> p k n", p=128)
        wq = []
        for q_ in range(2):
            w1, Bw1 = W1r.next()
            w2, Bw2 = W2r.next()
            LOADW(w1, w_f1_v[:, :, q_ * 1024:(q_ + 1) * 1024], Bw1)
            LOADW(w2, w_f2_v[:, q_ * 8:(q_ + 1) * 8, :], Bw2)
            wq.append((w1, Bw1, w2, Bw2))
        for T in range(16):
            norm_to_hT(R[:, T, :], B_R[T], hbring, T % 2, hT2[:, :, T * 128:(T + 1) * 128], B_hT2[T // 2], "act")
        uTr = Ring("uT", 3, [128, 8, 256], BF16)
        rlr = Ring("rl", 3, [128, 256], F32)
        ucnt = [0]

        def u_part(w1, Bw1, tg):
            uT, BuT = uTr.next()
            for j in range(8):
                bk = ucnt[0] % 4
                ucnt[0] += 1
                for k in range(8):
                    MM(bank(bk, 256), w1[:, k, j * 128:(j + 1) * 128], hT2[:, k, tg * 256:(tg + 1) * 256],
                       k == 0, k == 7, [Bw1, B_hT2[tg]], [PB[bk]])
                rl, Brl = rlr.next()
                ACTV(rl, bank(bk, 256), AF.Relu, [PB[bk]], [Brl])
                TT("pool" if j % 2 else "dve", uT[:, j, :], rl, rl, ALU.mult, [Brl], [BuT])
            return uT, BuT

        def y_part(w2, Bw2, tg, uT, BuT):
            for t in range(2):
                T = tg * 2 + t
                for hf in range(2):
                    bk = 4 + t * 2 + hf
                    for j in range(8):
                        MM(bank(bk), uT[:, j, t * 128:(t + 1) * 128], w2[:, j, hf * 512:(hf + 1) * 512],
                           j == 0, j == 7, [BuT, Bw2], [PB[bk]])
                    dst = R[:, T, hf * 512:(hf + 1) * 512]
                    TT("dve", dst, bank(bk), dst, ALU.add, [PB[bk], B_R[T]], [B_R[T]])

        work = [(q_, tg) for q_ in range(4) for tg in range(8)]
        prev = None
        for (q_, tg) in work:
            if q_ >= 2 and tg == 0:
                w1, Bw1 = W1r.next()
                w2, Bw2 = W2r.next()
                LOADW(w1, w_f1_v[:, :, q_ * 1024:(q_ + 1) * 1024], Bw1)
                LOADW(w2, w_f2_v[:, q_ * 8:(q_ + 1) * 8, :], Bw2)
                wq.append((w1, Bw1, w2, Bw2))
            w1, Bw1, w2, Bw2 = wq[q_]
            cur = (w2, Bw2, tg) + u_part(w1, Bw1, tg)
            if prev is not None:
                y_part(*prev)
            prev = cur
        y_part(*prev)
        P.barrier()
        W1r.release()
        W2r.release()
        uTr.release()
        rlr.release()

        P.dma("sp", gbc, g_ple.partition_broadcast(128), B_gbc, True)
        Wpg = AR.alloc("Wpg", [128, 8, D], BF16)
        Wp = AR.alloc("Wp", [128, 2, D], BF16)
        B_Wpg, B_Wp = Buf("Wpg"), Buf("Wp")
        LOADW(Wpg, w_pg.rearrange("(k p) n -> p k n", p=128), B_Wpg)
        LOADW(Wp, w_p.rearrange("(k p) n -> p k n", p=128), B_Wp)
        pT = AR.alloc("pT", [128, 2, NOWN], BF16)
        B_pT = [Buf("pT%d" % i) for i in range(16)]
        ppr = Ring("ppf", 2, [128, 256], F32)
        pbr = Ring("ppb", 2, [128, 256], BF16)
        for T in range(16):
            norm_to_hT(R[:, T, :], B_R[T], hbring, T % 2, hT2[:, :, T * 128:(T + 1) * 128], B_hT2[T // 2], "act")
            pf, Bpf = ppr.next()
            P.dma("sp", pf, pp[T * 128:(T + 1) * 128, :], Bpf, True)
            pb, Bpb = pbr.next()
            CP("pool", pb, pf, [Bpf], [Bpb])
            ptp = bankb(2 + T % 2)
            for k in range(2):
                TR(ptp[:, k * 128:(k + 1) * 128], pb[:, k * 128:(k + 1) * 128], [Bpb], [PB[2 + T % 2]])
            CP("dve", pT[:, :, T * 128:(T + 1) * 128], ptp[:, 0:256].rearrange("p (k n) -> p k n", k=2),
               [PB[2 + T % 2]], [B_pT[T]])
        sgr = Ring("sg", 2, [128, 512], F32)
        mpr = Ring("mp", 2, [128, 512], F32)
        for T in range(16):
            for hf in range(2):
                a = (2 * T + hf) % 2
                bg_, be_ = 4 + a, 6 + a
                cs = slice(hf * 512, (hf + 1) * 512)
                for k in range(8):
                    MM(bank(bg_), hT2[:, k, T * 128:(T + 1) * 128], Wpg[:, k, cs], k == 0, k == 7,
                       [B_hT2[T // 2], B_Wpg], [PB[bg_]])
                for k in range(2):
                    MM(bank(be_), pT[:, k, T * 128:(T + 1) * 128], Wp[:, k, cs], k == 0, k == 1,
                       [B_pT[T], B_Wp], [PB[be_]])
                sg, Bsg = sgr.next()
                ACTV(sg, bank(bg_), AF.Sigmoid, [PB[bg_]], [Bsg])
                mp, Bmp = mpr.next()
                TT("dve", mp, bank(be_), sg, ALU.mult, [PB[be_], Bsg], [Bmp])
                dst = R[:, T, cs]
                TT("pool", dst, dst, mp, ALU.add, [Bmp, B_R[T]], [B_R[T]])
        gfin = AR.alloc("gfin", [128, D], F32)
        B_gfin = Buf("gfin")
        P.dma("sp", gfin, g_fin.partition_broadcast(128), B_gfin, True)
        for T in range(16):
            rs, Brs = rms_rstd(R[:, T, :], B_R[T], D)
            STT("dve", R[:, T, :], R[:, T, :], rs[:, 0:1], gfin, ALU.mult, ALU.mult, [B_R[T], Brs, B_gfin], [B_R[T]])
            P.dma("sp", out[T * 128:(T + 1) * 128, :], R[:, T, :], B_R[T], False)
        P.barrier()
        P.emit(st)
    return nc


def _t5_bucket(rel):
    nb = 16
    ret = (rel > 0).astype(np.int32) * nb
    n = np.abs(rel)
    max_exact = nb // 2
    large = max_exact + (np.log(np.maximum(n, 1) / max_exact) / math.log(1024 / max_exact)
                         * (nb - max_exact)).astype(np.int32)
    large = np.minimum(large, nb - 1)
    return ret + np.where(n < max_exact, n, large).astype(np.int32)


def _rope_table():
    half = 64
    inv_freq = np.power(np.float32(10000.0), -np.arange(0, half, 2, dtype=np.float32) / np.float32(half)).astype(np.float32)
    pos = np.arange(S)
    row = (pos // 64).astype(np.float32)
    col = (pos % 64).astype(np.float32)
    ar = (row[:, None] * inv_freq[None, :]).astype(np.float32)
    ac = (col[:, None] * inv_freq[None, :]).astype(np.float32)
    cr, sr, cc, sc = np.cos(ar), np.sin(ar), np.cos(ac), np.sin(ac)
    C = np.concatenate([cr, cr, cc, cc], axis=1)
    Sg = np.concatenate([-sr, sr, -sc, sc], axis=1)
    return np.concatenate([C, Sg], axis=1).astype(np.float32)


def _bias_tables(rel_bias):
    p = np.arange(128)[:, None]
    q = np.arange(128)[None, :]
    tabs = np.empty((6, 128, 512), np.float32)
    for g, c in enumerate(DIL):
        for kind in range(2):
            off = p - 64 + 128 * kind - q
            bucket = _t5_bucket(off * c)
            band = np.abs(off) <= 64
            for hh in range(4):
                vals = rel_bias[bucket, 4 * g + hh]
                tabs[2 * g + kind][:, hh * 128:(hh + 1) * 128] = np.where(band, vals, np.float32(NEG))
    return tabs


def _kmask(r0):
    km = np.zeros((128, 69), np.float32)
    p = np.arange(128)
    for g, c in enumerate(DIL):
        for r in range(c):
            for j in range(NJ[g]):
                th = 1024 + r - 64 * c + 128 * c * j + c * p
                ab = r0 - 1024 + th
                valid = (ab >= 0) & (ab < S)
                km[:, GOFF[g] + r * NJ[g] + j] = np.where(valid, 0.0, NEG)
    return km


def make_in_maps(inputs):
    f = lambda a: np.ascontiguousarray(np.asarray(a, dtype=np.float32))
    x = f(inputs["x"])
    p = f(inputs["p"])
    rope = _rope_table()
    biasT = _bias_tables(f(inputs["rel_bias"]))
    shared = {
        "biasT": biasT,
        "w_in": f(inputs["w_in"][0]), "w_out_a": f(inputs["w_out_a"][0]), "w_out_b": f(inputs["w_out_b"][0]),
        "w_out": f(inputs["w_out"][0]), "w_ff1": f(inputs["w_ff1"][0]), "w_ff2": f(inputs["w_ff2"][0]),
        "w_ple_gate": f(inputs["w_ple_gate"][0]), "w_ple": f(inputs["w_ple"][0]),
        "g_mix": f(inputs["norm_mix_g"][0]).reshape(1, D), "g_mlp": f(inputs["norm_mlp_g"][0]).reshape(1, D),
        "g_ple": f(inputs["norm_ple_g"][0]).reshape(1, D), "g_fin": f(inputs["final_norm_g"]).reshape(1, D),
        "g_q": f(inputs["q_norm_g"][0]).reshape(1, 128), "g_k": f(inputs["k_norm_g"][0]).reshape(1, 128),
        "bgate": f(f(inputs["b_gate"][0]).reshape(16, 128).T),
    }
    maps = []
    for core in range(8):
        b, r0 = core // 4, (core % 4) * NOWN
        xsr = np.ascontiguousarray(np.roll(x[b], -r0, axis=0))
        xhh = np.zeros((4096, D), np.float32)
        lo, hi = r0 - 1024, r0 + 3072
        a0, a1 = max(lo, 0), min(hi, S)
        xhh[a0 - lo:a1 - lo] = x[b, a0:a1]
        m = dict(shared)
        m.update({
            "xs": xsr, "xh": xhh, "pp": np.ascontiguousarray(p[0, b, r0:r0 + NOWN]),
            "rope": np.ascontiguousarray(np.roll(rope, -r0, axis=0)), "kmask": _kmask(r0),
        })
        maps.append(m)
    return maps


_NC_CACHE = {}


def kernel(**inputs):
    if "full" not in _NC_CACHE:
        _NC_CACHE["full"] = build("full")
    nc = _NC_CACHE["full"]
    maps = make_in_maps(inputs)
    res = run_bass_kernel_spmd(nc, maps, core_ids=list(range(8)))
    o = np.empty((2, S, D), np.float32)
    for core in range(8):
        b, r0 = core // 4, (core % 4) * NOWN
        o[b, r0:r0 + NOWN] = res.results[core]["out"]
    return o
```

```python
import math
from contextlib import ExitStack
import numpy as np
import concourse.bass as bass
import concourse.mybir as mybir
from concourse.bass_utils import run_bass_kernel_spmd

F32 = mybir.dt.float32
BF16 = mybir.dt.bfloat16
AF = mybir.ActivationFunctionType
ALU = mybir.AluOpType

ENGS = ("pe", "act", "dve", "pool", "sp")
S = 8192
D = 1024
NOWN = 2048
NEG = -100.0
EPS = 1e-6
DIL = (1, 4, 16)
NJ = (17, 5, 2)
GOFF = (0, 17, 37)


class Buf:
    __slots__ = ("name", "w", "rd", "dsem", "dcnt")

    def __init__(self, name):
        self.name = name
        self.w = None
        self.rd = {}
        self.dsem = None
        self.dcnt = 0


class Ins:
    __slots__ = ("eng", "fn", "waits", "flag", "dbuf", "seq")

    def __init__(self, eng, fn, seq):
        self.eng = eng
        self.fn = fn
        self.waits = []
        self.flag = False
        self.dbuf = None
        self.seq = seq


class Prog:
    def __init__(self, nc):
        self.nc = nc
        self.q = {e: [] for e in ENGS}
        self.waited = {e: {} for e in ENGS}
        self.dma_bufs = []

    def _need(self, ins, tok):
        if tok is None:
            return
        if tok[0] == "e":
            _, eng, seq = tok
            if eng == "pe" and ins.eng == "pe":
                return
            key = ("e", eng)
            val = seq
        else:
            _, buf, cnt = tok
            key = ("d", id(buf))
            val = cnt
        w = self.waited[ins.eng]
        if w.get(key, -1) >= val:
            return
        w[key] = val
        ins.waits.append(tok)
        if tok[0] == "e":
            self.q[tok[1]][tok[2]].flag = True

    def _deps(self, ins, reads, writes):
        for b in reads:
            self._need(ins, b.w)
        for b in writes:
            self._need(ins, b.w)
            for eng, seq in b.rd.items():
                self._need(ins, ("e", eng, seq))
            if b.dcnt:
                self._need(ins, ("d", b, b.dcnt))

    def op(self, eng, fn, reads=(), writes=()):
        ins = Ins(eng, fn, len(self.q[eng]))
        self._deps(ins, reads, writes)
        self.q[eng].append(ins)
        tok = ("e", eng, ins.seq)
        for b in reads:
            b.rd[eng] = ins.seq
        for b in writes:
            b.w = tok
            b.rd = {}
        return ins

    def dma(self, queue, out, in_, track, is_write, reads=(), writes=()):
        ins = Ins(queue, (lambda e: e.dma_start(out=out, in_=in_)), len(self.q[queue]))
        rs = list(reads)
        ws = list(writes)
        (ws if is_write else rs).append(track)
        self._deps(ins, rs, ws)
        if track.dsem is None:
            track.dsem = True
            self.dma_bufs.append(track)
        track.dcnt += 1
        ins.dbuf = track
        self.q[queue].append(ins)
        if is_write:
            track.w = ("d", track, track.dcnt)
            track.rd = {}
        return ins

    def barrier(self):
        last = {}
        for e in ("pe", "act", "dve", "pool"):
            s = len(self.q[e]) - 1
            while s >= 0 and (self.q[e][s].dbuf is not None or self.q[e][s].fn is None):
                s -= 1
            last[e] = s
        dl = [(b, b.dcnt) for b in self.dma_bufs if b.dcnt]
        for e in ENGS:
            ins = Ins(e, None, len(self.q[e]))
            for pe_, s in last.items():
                if s >= 0:
                    self._need(ins, ("e", pe_, s))
            for b, c in dl:
                self._need(ins, ("d", b, c))
            self.q[e].append(ins)

    def emit(self, stack):
        nc = self.nc
        esem = {e: stack.enter_context(nc.semaphore("ms_" + e)) for e in ("pe", "act", "dve", "pool")}
        for i, b in enumerate(self.dma_bufs):
            b.dsem = stack.enter_context(nc.semaphore("d%d_%s" % (i, b.name)))
        rank = {}
        for e in ("pe", "act", "dve", "pool"):
            r = 0
            for ins in self.q[e]:
                if ins.flag:
                    r += 1
                    rank[(e, ins.seq)] = r
        q = self.q

        def run(e, h):
            for ins in q[e]:
                for tok in ins.waits:
                    if tok[0] == "e":
                        h.wait_ge(esem[tok[1]], rank[(tok[1], tok[2])])
                    else:
                        h.wait_ge(tok[1].dsem, 16 * tok[2])
                if ins.fn is None:
                    continue
                bi = ins.fn(h)
                if ins.dbuf is not None:
                    bi.then_inc(ins.dbuf.dsem, 16)
                elif ins.flag:
                    bi.then_inc(esem[e], 1)

        block = stack.enter_context(nc.Block())

        @block.tensor
        def _(t):
            run("pe", t)

        @block.scalar
        def _(t):
            run("act", t)

        @block.vector
        def _(t):
            run("dve", t)

        @block.gpsimd
        def _(t):
            run("pool", t)

        @block.sync
        def _(t):
            run("sp", t)


class Arena:
    def __init__(self, nc, nwords):
        self.ap = nc.alloc_sbuf_tensor("arena", [128, nwords], F32).ap()
        self.free = [(0, nwords)]
        self.live = {}

    def alloc(self, name, shape, dt, top=False):
        n = int(np.prod(shape[1:]))
        nbytes = n * (4 if dt == F32 else 2)
        w = (nbytes + 31) // 32 * 8
        order = range(len(self.free) - 1, -1, -1) if top else range(len(self.free))
        for i in order:
            o, sz = self.free[i]
            if sz >= w:
                if top:
                    off = o + sz - w
                    if sz == w:
                        self.free.pop(i)
                    else:
                        self.free[i] = (o, sz - w)
                else:
                    off = o
                    if sz == w:
                        self.free.pop(i)
                    else:
                        self.free[i] = (o + w, sz - w)
                break
        else:
            raise RuntimeError("arena full allocating %s (%d words); free=%s" % (name, w, self.free))
        assert name not in self.live, name
        self.live[name] = (off, w)
        a = self.ap[0:shape[0], off:off + w]
        if dt != F32:
            a = a.bitcast(dt)
        a = a[:, 0:n]
        if len(shape) == 3:
            a = a.rearrange("p (a b) -> p a b", a=shape[1])
        elif len(shape) == 4:
            a = a.rearrange("p (a b c) -> p a b c", a=shape[1], b=shape[2])
        return a

    def release(self, *names):
        for name in names:
            off, w = self.live.pop(name)
            self.free.append((off, w))
        self.free.sort()
        m = []
        for o, sz in self.free:
            if m and m[-1][0] + m[-1][1] == o:
                m[-1] = (m[-1][0], m[-1][1] + sz)
            else:
                m.append((o, sz))
        self.free = m


def build(stage="full"):
    nc = bass.Bass("TRN2", target_bir_lowering=False)

    def din(name, shape):
        return nc.dram_tensor(name, list(shape), F32, kind="ExternalInput").ap()

    xs = din("xs", [S, D])
    xh = din("xh", [4096, D])
    pp = din("pp", [NOWN, 256])
    rope = din("rope", [S, 256])
    kmask_d = din("kmask", [128, 69])
    biasT_d = din("biasT", [6, 128, 512])
    w_in = din("w_in", [D, 5888])
    w_oa = din("w_out_a", [D, D])
    w_ob = din("w_out_b", [256, D])
    w_o = din("w_out", [D, D])
    w_f1 = din("w_ff1", [D, 4096])
    w_f2 = din("w_ff2", [4096, D])
    w_pg = din("w_ple_gate", [D, D])
    w_p = din("w_ple", [256, D])
    g_mix = din("g_mix", [1, D])
    g_mlp = din("g_mlp", [1, D])
    g_ple = din("g_ple", [1, D])
    g_fin = din("g_fin", [1, D])
    g_q = din("g_q", [1, 128])
    g_k = din("g_k", [1, 128])
    bgate_d = din("bgate", [128, 16])
    out = nc.dram_tensor("out", [NOWN, D], F32, kind="ExternalOutput").ap()
    vscr = nc.dram_tensor("vscr", [4096, 768], BF16).ap()
    dbg = None
    if stage == "B":
        dbg = nc.dram_tensor("dbg", [128, 2 * NOWN], F32, kind="ExternalOutput").ap()
    elif stage == "A":
        dbg = nc.dram_tensor("dbg", [128, 8 * NOWN], F32, kind="ExternalOutput").ap()

    w_in_v = w_in.rearrange("(k p) n -> p k n", p=128)

    st = ExitStack()
    with st:
        P = Prog(nc)
        AR = Arena(nc, 50400)
        ps = st.enter_context(nc.psum_tensor("ps", [128, 4096], F32)).ap()
        PB = [Buf("ps%d" % i) for i in range(8)]

        def bank(i, n=512):
            return ps[:, i * 512:i * 512 + n]

        def bankb(i, n=1024):
            return ps[:, i * 512:(i + 1) * 512].bitcast(BF16)[:, 0:n]

        def MM(o, lhsT, rhs, start, stop, reads, writes):
            P.op("pe", lambda e: e.matmul(o, lhsT, rhs, start=start, stop=stop), reads, writes)

        def TR(o, i, reads, writes):
            P.op("pe", lambda e: e.transpose(o, i, ident), list(reads) + [B_ident], writes)

        def ACTV(o, i, func, reads, writes, bias=0.0, scale=1.0, accum=None):
            P.op("act", lambda e: e.activation(o, i, func, bias=bias, scale=scale, accum_out=accum), reads, writes)

        def TT(eng, o, a, b, op, reads, writes):
            P.op(eng, lambda e: e.tensor_tensor(o, a, b, op), reads, writes)

        def STT(eng, o, a, sc, b, op0, op1, reads, writes):
            P.op(eng, lambda e: e.scalar_tensor_tensor(o, a, sc, b, op0, op1), reads, writes)

        def CP(eng, o, i, reads, writes):
            if eng == "act":
                P.op("act", lambda e: e.copy(o, i), reads, writes)
            else:
                P.op(eng, lambda e: e.tensor_copy(o, i), reads, writes)

        def RECIP(o, i, reads, writes):
            P.op("dve", lambda e: e.reciprocal(o, i), reads, writes)

        def MEMSET(eng, o, v, writes):
            P.op(eng, lambda e: e.memset(o, v), (), writes)

        def LOADW(dst, src, buf):
            for k in range(dst.shape[1]):
                P.dma("pool", dst[:, k, :], src[:, k, :], buf, True)

        class Ring:
            def __init__(self, name, n, shape, dt, top=False):
                self.aps = [AR.alloc("%s%d" % (name, i), shape, dt, top=top) for i in range(n)]
                self.bufs = [Buf("%s%d" % (name, i)) for i in range(n)]
                self.names = ["%s%d" % (name, i) for i in range(n)]
                self.i = 0

            def next(self):
                k = self.i % len(self.aps)
                self.i += 1
                return self.aps[k], self.bufs[k]

            def release(self):
                AR.release(*self.names)

        ident = AR.alloc("ident", [128, 128], BF16)
        B_ident = Buf("ident")
        identf = AR.alloc("identf", [128, 128], F32)
        B_identf = Buf("identf")
        MEMSET("pool", identf, 0.0, [B_identf])
        P.op("pool", lambda e: e.affine_select(identf, identf, [[-1, 128]], ALU.not_equal, 1.0, base=0,
                                               channel_multiplier=1), [B_identf], [B_identf])
        CP("dve", ident, identf, [B_identf], [B_ident])
        ones = AR.alloc("ones", [128, 128], BF16)
        B_ones = Buf("ones")
        MEMSET("pool", ones, 1.0, [B_ones])
        gq_bc = AR.alloc("gq_bc", [128, 128], F32)
        gk_bc = AR.alloc("gk_bc", [128, 128], F32)
        B_gq = Buf("gq")
        B_gk = Buf("gk")
        P.dma("sp", gq_bc, g_q.partition_broadcast(128), B_gq, True)
        P.dma("sp", gk_bc, g_k.partition_broadcast(128), B_gk, True)
        bgate = AR.alloc("bgate", [128, 16], F32)
        B_bgate = Buf("bgate")
        P.dma("sp", bgate, bgate_d, B_bgate, True)
        gbc = AR.alloc("gbc", [128, D], F32)
        B_gbc = Buf("gbc")
        P.dma("sp", gbc, g_mix.partition_broadcast(128), B_gbc, True)
        stat = AR.alloc("stat", [128, 128], F32)
        sqj = AR.alloc("sqj", [128, D], F32)
        B_sqj = Buf("sqj")
        statbufs = [Buf("stat%d" % i) for i in range(32)]
        stat_i = [0]

        def stat_slot(n=1):
            assert n <= 4
            k = stat_i[0] % 32
            stat_i[0] += 1
            return stat[:, 4 * k:4 * k + n], statbufs[k]

        def rms_rstd(x_ap, Bx, width, ncols=1, xs_list=None):
            ss, Bss = stat_slot(ncols)
            srcs = xs_list if xs_list is not None else [x_ap]
            for c, src in enumerate(srcs):
                ACTV(sqj[:, 0:width], src, AF.Square, [Bx], [B_sqj, Bss], accum=ss[:, c:c + 1])
            rs, Brs = stat_slot(ncols)
            ACTV(rs, ss, AF.Sqrt, [Bss], [Brs], bias=EPS, scale=1.0 / width)
            RECIP(rs, rs, [Brs], [Brs])
            return rs, Brs

        def norm_to_hT(x_ap, Bx, hb_ring, trbank, dst_ap, Bdst, copy_eng):
            rs, Brs = rms_rstd(x_ap, Bx, D)
            hb, Bhb = hb_ring.next()
            STT("dve", hb, x_ap, rs[:, 0:1], gbc, ALU.mult, ALU.mult, [Bx, Brs, B_gbc], [Bhb])
            pt = bankb(trbank)
            for k in range(8):
                TR(pt[:, k * 128:(k + 1) * 128], hb[:, k * 128:(k + 1) * 128], [Bhb], [PB[trbank]])
            CP(copy_eng, dst_ap, pt.rearrange("p (k n) -> p k n", k=8), [PB[trbank]], [Bdst])

        def rms_rstd_g(Bx, width, srcs):
            ncols = len(srcs)
            ss, Bss = stat_slot(ncols)
            for c_, src in enumerate(srcs):
                ACTV(sqj[:, 0:width], src, AF.Square, [Bx], [B_sqj, Bss], accum=ss[:, c_:c_ + 1])
            yield None
            rs, Brs = stat_slot(ncols)
            ACTV(rs, ss, AF.Sqrt, [Bss], [Brs], bias=EPS, scale=1.0 / width)
            yield None
            RECIP(rs, rs, [Brs], [Brs])
            yield None
            yield (rs, Brs)

        def norm_g(x_ap, Bx, trbank, dst_ap, Bdst, pre=None):
            if pre is not None:
                pre()
                yield
            rsb = None
            for v in rms_rstd_g(Bx, D, [x_ap]):
                if v is None:
                    yield
                else:
                    rsb = v
            rs, Brs = rsb
            hb, Bhb = hbring.next()
            STT("dve", hb, x_ap, rs[:, 0:1], gbc, ALU.mult, ALU.mult, [Bx, Brs, B_gbc], [Bhb])
            yield
            pt = bankb(trbank)
            for k in range(8):
                TR(pt[:, k * 128:(k + 1) * 128], hb[:, k * 128:(k + 1) * 128], [Bhb], [PB[trbank]])
            yield
            CP("act", dst_ap, pt.rearrange("p (k n) -> p k n", k=8), [PB[trbank]], [Bdst])
            yield

        def lockstep(gens, n):
            pend = list(gens)
            active = []
            while pend or active:
                while pend and len(active) < n:
                    active.append(pend.pop(0))
                for g_ in list(active):
                    try:
                        next(g_)
                    except StopIteration:
                        active.remove(g_)

        KBT = AR.alloc("KBT", [128, 6, 4096], BF16)
        QBT = AR.alloc("QBT", [128, 6, NOWN], BF16)
        B_KBT = Buf("KBT")
        B_QBT = Buf("QBT")
        WB = AR.alloc("WB", [128, 8, 2304], BF16)
        B_WB = Buf("WB")
        LOADW(WB, w_in_v[:, :, 1536:3840], B_WB)
        xring = Ring("xr", 3, [128, D], F32)
        hbring = Ring("hb", 2, [128, D], BF16)
        hTg = Ring("hTg", 2, [128, 8, 512], BF16)
        vtr = Ring("vt", 2, [128, 768], BF16)
        for G in range(0 if stage[0] == "x" else (8 if stage != "1a1" else 1)):
            hT, BhT = hTg.next()
            def tile1a_g(T, t):
                xa, Bx = xring.next()
                P.dma("sp", xa, xh[T * 128:(T + 1) * 128, :], Bx, True)
                yield
                for _ in norm_g(xa, Bx, T % 2, hT[:, :, t * 128:(t + 1) * 128], BhT):
                    yield

            lockstep([tile1a_g(G * 4 + t, t) for t in range(4)], 2)
            for c in range(6):
                bk = 2 + (c % 2)
                for k in range(8):
                    MM(bank(bk), WB[:, k, 768 + c * 128:768 + (c + 1) * 128], hT[:, k, :], k == 0, k == 7,
                       [B_WB, BhT], [PB[bk]])
                CP("dve", KBT[:, c, G * 512:(G + 1) * 512], bank(bk), [PB[bk]], [B_KBT])
            if 2 <= G < 6:
                for c in range(6):
                    bk = 2 + (c % 2)
                    for k in range(8):
                        MM(bank(bk), WB[:, k, c * 128:(c + 1) * 128], hT[:, k, :], k == 0, k == 7,
                           [B_WB, BhT], [PB[bk]])
                    ACTV(QBT[:, c, (G - 2) * 512:(G - 1) * 512], bank(bk), AF.Copy, [PB[bk]], [B_QBT], scale=0.125)
            for t in range(4):
                T = G * 4 + t
                b0 = 4 + 2 * (t % 2)
                for k in range(8):
                    MM(bank(b0), hT[:, k, t * 128:(t + 1) * 128], WB[:, k, 1536:2048], k == 0, k == 7,
                       [B_WB, BhT], [PB[b0]])
                for k in range(8):
                    MM(bank(b0 + 1, 256), hT[:, k, t * 128:(t + 1) * 128], WB[:, k, 2048:2304], k == 0, k == 7,
                       [B_WB, BhT], [PB[b0 + 1]])
                vt, Bvt = vtr.next()
                CP("dve", vt[:, 0:512], bank(b0), [PB[b0]], [Bvt])
                CP("act", vt[:, 512:768], bank(b0 + 1, 256), [PB[b0 + 1]], [Bvt])
                P.dma("sp", vscr[T * 128:(T + 1) * 128, :], vt, Bvt, False)
        P.barrier()
        if stage in ("1a", "1a1"):
            P.emit(st)
            return nc
        AR.release("WB")
        xring.release()
        hbring.release()
        hTg.release()
        vtr.release()

        BT = AR.alloc("BT", [128, 2, NOWN], BF16, top=True)
        B_BT = Buf("BT")
        Vext = AR.alloc("Vext", [128, 32, 4, 128], BF16)
        B_Vext = Buf("Vext")
        Vraw = AR.alloc("Vraw", [128, 32, 256], BF16)
        B_Vraw = Buf("Vraw")
        Bacc = AR.alloc("Bacc", [128, 4, NOWN], F32)
        B_Bacc = Buf("Bacc")
        kmask = AR.alloc("kmask", [128, 69], F32)
        B_kmask = Buf("kmask")
        P.dma("sp", kmask, kmask_d, B_kmask, True)
        biasr = Ring("bias", 2, [128, 2, 512], F32)
        s2r = Ring("s2", 2, [128, 512], F32)
        ptr = Ring("ptb", 3, [128, 512], BF16)
        Vext3 = Vext.rearrange("p j h n -> p (j h) n")
        MEMSET("pool", Vext3[:, :, 64:128], 1.0, [B_Vext])
        qpr = Ring("qpad", 3, [128, 2, 256], BF16)
        for qa_, qb_ in zip(qpr.aps, qpr.bufs):
            MEMSET("pool", qa_, 0.0, [qb_])
        nblk = 0
        for g in range(3):
            c = DIL[g]
            nj = NJ[g]
            Lq = NOWN // c
            bias_ap, Bbias = biasr.next()
            P.dma("sp", bias_ap, biasT_d[2 * g:2 * g + 2].rearrange("k p n -> p k n"), Bbias, True)
            for r in range(c):
                base = 1024 + r - 64 * c
                for j0 in range(0, nj, 4):
                    j1 = min(nj, j0 + 4)
                    b0_ = base + c * 128 * j0
                    src = vscr[b0_:b0_ + c * (128 * (j1 - j0) - 1) + 1:c, g * 256:(g + 1) * 256].rearrange(
                        "(j p) n -> p j n", p=128)
                    P.dma("sp", Vraw[:, r * nj + j0:r * nj + j1, :], src, B_Vraw, True)
            CP("pool", Vext3[:, 0:c * nj * 4, 0:64],
               Vraw[:, 0:c * nj, :].rearrange("p j (h d) -> p (j h) d", h=4), [B_Vraw], [B_Vext])
            xl = int(stage[1]) if stage[0] == "x" else 9
            for r in range(c):
                if stage == "1b_a" or (stage in ("1b_b",) and g > 0) or (stage[0] == "x" and g > 0):
                    break
                for i in range(Lq // 128 if stage[0] != "x" else 2):
                    sb = [2 * (nblk % 2), 2 * (nblk % 2) + 1]
                    ob = 4 + (nblk % 2)
                    nblk += 1
                    pts = []
                    qstart = r + c * 128 * i
                    qp, Bqp = qpr.next()
                    for cc in range(2):
                        ch = 2 * g + cc
                        CP("act", qp[0:64, cc, 0:128], QBT[0:64, ch, qstart:qstart + c * 127 + 1:c], [B_QBT], [Bqp])
                        CP("act", qp[64:128, cc, 128:256], QBT[64:128, ch, qstart:qstart + c * 127 + 1:c], [B_QBT], [Bqp])
                    for kind in range(2):
                        j = i + kind
                        kstart = 1024 + r + c * (128 * j - 64)
                        for cc in range(2):
                            ch = 2 * g + cc
                            MM(bank(sb[kind])[:, cc * 256:(cc + 1) * 256],
                               KBT[:, ch, kstart:kstart + c * 127 + 1:c], qp[:, cc, :], True, True,
                               [B_KBT, Bqp], [PB[sb[kind]]])
                        if xl < 2:
                            continue
                        s2, Bs2 = s2r.next()
                        TT("dve", s2, bank(sb[kind]), bias_ap[:, kind, :], ALU.add, [PB[sb[kind]], Bbias], [Bs2])
                        pt, Bpt = ptr.next()
                        tile_idx = GOFF[g] + r * nj + j
                        ACTV(pt, s2, AF.Exp, [Bs2, B_kmask], [Bpt], bias=kmask[:, tile_idx:tile_idx + 1])
                        pts.append((pt, Bpt))
                    if xl < 3:
                        continue
                    for hh in range(4):
                        for kind in range(2):
                            j = i + kind
                            slot = r * nj + j
                            lhsT = Vext[:, slot, hh, :]
                            MM(bank(ob)[:, hh * 128:(hh + 1) * 128], lhsT, pts[kind][0][:, hh * 128:(hh + 1) * 128],
                               kind == 0, kind == 1, [B_Vext, pts[kind][1]], [PB[ob]])
                    if xl < 4:
                        continue
                    qstart = r + c * 128 * i
                    dst = Bacc[:, :, qstart:qstart + c * 127 + 1:c]
                    srcp = bank(ob).rearrange("p (h n) -> p h n", h=4)
                    if g == 0:
                        CP("dve", dst, srcp, [PB[ob]], [B_Bacc])
                    else:
                        TT("dve", dst, srcp, dst, ALU.add, [PB[ob], B_Bacc], [B_Bacc])
        P.barrier()
        if stage in ("1b_a", "1b_b", "B2") or stage[0] == "x":
            P.emit(st)
            return nc
        AR.release("Vext", "Vraw")
        denlo = AR.alloc("denlo", [64, 4, NOWN], F32)
        B_denlo = Buf("denlo")
        for hh in range(4):
            P.dma("sp", denlo[:, hh, :], Bacc[64:128, hh, :], B_denlo, True, reads=[B_Bacc])
        BTo = AR.alloc("BTo", [64, 2, NOWN], BF16)
        B_BTo = Buf("BTo")
        for hh in range(4):
            RECIP(denlo[:, hh, :], denlo[:, hh, :], [B_denlo], [B_denlo])
            if hh % 2 == 0:
                TT("dve", BT[0:64, hh // 2, :], Bacc[0:64, hh, :], denlo[:, hh, :], ALU.mult, [B_Bacc, B_denlo], [B_BT])
            else:
                TT("dve", BTo[:, hh // 2, :], Bacc[0:64, hh, :], denlo[:, hh, :], ALU.mult, [B_Bacc, B_denlo], [B_BTo])
        for k2 in range(2):
            P.dma("sp", BT[64:128, k2, :], BTo[:, k2, :], B_BT, True, reads=[B_BTo])
        if stage == "B":
            P.barrier()
            AR.release("Bacc")
            dtmp = AR.alloc("dtmpB", [128, 2 * NOWN], F32)
            B_dtmp = Buf("dtmpB")
            CP("dve", dtmp, BT.rearrange("p a n -> p (a n)"), [B_BT], [B_dtmp])
            P.dma("sp", dbg[:, 0:2048], dtmp[:, 0:2048], B_dtmp, False)
            P.dma("sp", dbg[:, 2048:4096], dtmp[:, 2048:4096], B_dtmp, False)
            P.barrier()
            P.emit(st)
            return nc
        P.barrier()
        AR.release("denlo", "Bacc", "kmask", "KBT", "QBT", "BTo")
        qpr.release()
        biasr.release()
        s2r.release()
        ptr.release()

        KAT = AR.alloc("KAT", [128, 2, S], BF16)
        VA = AR.alloc("VA", [128, 64, 256], BF16)
        QAT = AR.alloc("QAT", [128, 8, NOWN], BF16)
        B_KAT = [Buf("KAT%d" % i) for i in range(16)]
        B_VA = [Buf("VA%d" % i) for i in range(16)]
        B_QAT = [Buf("QAT%d" % i) for i in range(4)]
        WA = AR.alloc("WA", [128, 8, 1536], BF16)
        B_WA = Buf("WA")
        LOADW(WA, w_in_v[:, :, 0:1536], B_WA)
        xring = Ring("xr", 3, [128, D], F32)
        rpr = Ring("rp", 3, [128, 256], F32)
        hbring = Ring("hb", 2, [128, D], BF16)
        hTr = Ring("hTt", 3, [128, 8, 128], BF16)
        knr = Ring("kn", 2, [128, 512], F32)
        t1r = Ring("t1", 2, [128, 512], F32)
        t2r = Ring("t2", 2, [128, 512], F32)
        krr = Ring("kr", 2, [128, 512], BF16)

        def qk_post(src_bank, Bsrc, nh, g_bc, Bg, rp, Brp, dst_fn):
            W = nh * 128
            rs, Brs = rms_rstd(None, Bsrc, 128, ncols=nh, xs_list=[src_bank[:, h * 128:(h + 1) * 128] for h in range(nh)])
            kn, Bkn = knr.next()
            for h in range(nh):
                STT("dve", kn[:, h * 128:(h + 1) * 128], src_bank[:, h * 128:(h + 1) * 128], rs[:, h:h + 1], g_bc,
                    ALU.mult, ALU.mult, [Bsrc, Brs, Bg], [Bkn])
            t1, Bt1 = t1r.next()
            t2, Bt2 = t2r.next()
            knv = kn[:, 0:W].rearrange("p (h n) -> p h n", h=nh)
            Cb = rp[:, 0:128].unsqueeze(1).to_broadcast([128, nh, 128])
            TT("pool", t1[:, 0:W].rearrange("p (h n) -> p h n", h=nh), knv, Cb, ALU.mult, [Bkn, Brp], [Bt1])
            kn5 = kn[:, 0:W].rearrange("p (h a b d) -> p h a b d", h=nh, a=2, b=2)
            t25 = t2[:, 0:W].rearrange("p (h a b d) -> p h a b d", h=nh, a=2, b=2)
            S5 = rp[:, 128:256].rearrange("p (a b d) -> p a b d", a=2, b=2)
            for b_ in range(2):
                for a_ in range(2):
                    Sb = S5[:, a_, b_, :].unsqueeze(1).to_broadcast([128, nh, 32])
                    TT("pool", t25[:, :, a_, b_, :], kn5[:, :, a_, 1 - b_, :], Sb, ALU.mult, [Bkn, Brp], [Bt2])
            kr, Bkr = krr.next()
            TT("dve", kr[:, 0:W], t1[:, 0:W], t2[:, 0:W], ALU.add, [Bt1, Bt2], [Bkr])
            return kr, Bkr

        def qk_post_g(src_bank, Bsrc, nh, g_bc, Bg, rp, Brp, res):
            W = nh * 128
            rsb = None
            for v in rms_rstd_g(Bsrc, 128, [src_bank[:, h * 128:(h + 1) * 128] for h in range(nh)]):
                if v is None:
                    yield
                else:
                    rsb = v
            rs, Brs = rsb
            kn, Bkn = knr.next()
            for h in range(nh):
                STT("dve", kn[:, h * 128:(h + 1) * 128], src_bank[:, h * 128:(h + 1) * 128], rs[:, h:h + 1], g_bc,
                    ALU.mult, ALU.mult, [Bsrc, Brs, Bg], [Bkn])
            yield
            t1, Bt1 = t1r.next()
            t2, Bt2 = t2r.next()
            knv = kn[:, 0:W].rearrange("p (h n) -> p h n", h=nh)
            Cb = rp[:, 0:128].unsqueeze(1).to_broadcast([128, nh, 128])
            TT("pool", t1[:, 0:W].rearrange("p (h n) -> p h n", h=nh), knv, Cb, ALU.mult, [Bkn, Brp], [Bt1])
            kn5 = kn[:, 0:W].rearrange("p (h a b d) -> p h a b d", h=nh, a=2, b=2)
            t25 = t2[:, 0:W].rearrange("p (h a b d) -> p h a b d", h=nh, a=2, b=2)
            S5 = rp[:, 128:256].rearrange("p (a b d) -> p a b d", a=2, b=2)
            for b_ in range(2):
                for a_ in range(2):
                    Sb = S5[:, a_, b_, :].unsqueeze(1).to_broadcast([128, nh, 32])
                    TT("pool", t25[:, :, a_, b_, :], kn5[:, :, a_, 1 - b_, :], Sb, ALU.mult, [Bkn, Brp], [Bt2])
            yield
            kr, Bkr = krr.next()
            TT("dve", kr[:, 0:W], t1[:, 0:W], t2[:, 0:W], ALU.add, [Bt1, Bt2], [Bkr])
            res.append((kr, Bkr))
            yield

        def tile2_g(T):
            xa, Bx = xring.next()
            P.dma("sp", xa, xs[T * 128:(T + 1) * 128, :], Bx, True)
            rp, Brp = rpr.next()
            P.dma("sp", rp, rope[T * 128:(T + 1) * 128, :], Brp, True)
            hT, BhT = hTr.next()
            yield
            rsb = None
            for v in rms_rstd_g(Bx, D, [xa]):
                if v is None:
                    yield
                else:
                    rsb = v
            rs, Brs = rsb
            hb, Bhb = hbring.next()
            STT("dve", hb, xa, rs[:, 0:1], gbc, ALU.mult, ALU.mult, [Bx, Brs, B_gbc], [Bhb])
            yield
            trb = T % 2
            pt = bankb(trb)
            for k in range(8):
                TR(pt[:, k * 128:(k + 1) * 128], hb[:, k * 128:(k + 1) * 128], [Bhb], [PB[trb]])
            yield
            CP("act", hT, pt.rearrange("p (k n) -> p k n", k=8), [PB[trb]], [BhT])
            yield
            bkv = 2 + (T % 2)
            for k in range(8):
                MM(bank(bkv), hT[:, k, :], WA[:, k, 1024:1536], k == 0, k == 7, [BhT, B_WA], [PB[bkv]])
            own = T < 16
            par = T % 2
            yield
            CP("act", VA[:, T, :], bank(bkv)[:, 256:512], [PB[bkv]], [B_VA[T // 4]])
            res = []
            for _ in qk_post_g(bank(bkv)[:, 0:256], PB[bkv], 2, gk_bc, B_gk, rp, Brp, res):
                yield
            kr, Bkr = res[0]
            ptk = bankb(6)[:, par * 256:par * 256 + 256]
            for h in range(2):
                TR(ptk[:, h * 128:(h + 1) * 128], kr[:, h * 128:(h + 1) * 128], [Bkr], [PB[6]])
            yield
            CP("act", KAT[:, :, T * 128:(T + 1) * 128], ptk.rearrange("p (h n) -> p h n", h=2),
               [PB[6]], [B_KAT[T // 4]])
            if own:
                ptq = bankb(7)[:, par * 512:par * 512 + 512]
                bq = 4 + par
                for hq in range(2):
                    for k in range(8):
                        MM(bank(bq), hT[:, k, :], WA[:, k, hq * 512:(hq + 1) * 512], k == 0, k == 7,
                           [BhT, B_WA], [PB[bq]])
                    yield
                    res = []
                    for _ in qk_post_g(bank(bq), PB[bq], 4, gq_bc, B_gq, rp, Brp, res):
                        yield
                    qr, Bqr = res[0]
                    for h in range(4):
                        TR(ptq[:, h * 128:(h + 1) * 128], qr[:, h * 128:(h + 1) * 128], [Bqr], [PB[7]])
                    yield
                    CP("dve", QAT[:, hq * 4:(hq + 1) * 4, T * 128:(T + 1) * 128], ptq.rearrange("p (h n) -> p h n", h=4),
                       [PB[7]], [B_QAT[T // 4]])
                    yield
            yield

        NLOCK = 2
        pend = list(range(64))
        active = []
        while pend or active:
            while pend and len(active) < NLOCK:
                active.append(tile2_g(pend.pop(0)))
            for g_ in list(active):
                try:
                    next(g_)
                except StopIteration:
                    active.remove(g_)
        P.barrier()
        if stage == "P2":
            P.emit(st)
            return nc
        AR.release("WA")
        for r_ in (xring, rpr, hbring, hTr, knr, t1r, t2r, krr):
            r_.release()

        AT = AR.alloc("AT", [128, 8, NOWN], BF16, top=True)
        B_AT = [Buf("AT%d" % i) for i in range(4)]
        ptr = Ring("ptA", 3, [128, 1024], BF16)
        rdr = Ring("rden", 2, [128, 512], F32)
        scale_a = 128.0 ** -0.5
        for qg in range(4):
            for h in range(8):
                kv = h // 4
                it = qg * 8 + h
                ob = 4 + (it % 2)
                db = 6 + (it % 2)
                q_ap = QAT[:, h, qg * 512:(qg + 1) * 512]

                def qk(jp):
                    sb0 = 2 * (jp % 2)
                    for u in range(2):
                        kt = 2 * jp + u
                        MM(bank(sb0 + u), KAT[:, kv, kt * 128:(kt + 1) * 128], q_ap, True, True,
                           [B_KAT[kt // 4], B_QAT[qg]], [PB[sb0 + u]])

                def ex(jp):
                    sb0 = 2 * (jp % 2)
                    pt, Bpt = ptr.next()
                    ACTV(pt, ps[:, sb0 * 512:(sb0 + 2) * 512], AF.Exp, [PB[sb0], PB[sb0 + 1]], [Bpt], scale=scale_a)
                    return pt, Bpt

                def pv(jp, pt, Bpt):
                    for u in range(2):
                        kt = 2 * jp + u
                        MM(bank(ob), VA[:, kt, kv * 128:(kv + 1) * 128], pt[:, u * 512:(u + 1) * 512],
                           kt == 0, kt == 63, [B_VA[kt // 4], Bpt], [PB[ob]])
                        MM(bank(db), ones, pt[:, u * 512:(u + 1) * 512], kt == 0, kt == 63,
                           [B_ones, Bpt], [PB[db]])

                qk(0)
                for jp in range(32):
                    if jp + 1 < 32:
                        qk(jp + 1)
                    pt, Bpt = ex(jp)
                    pv(jp, pt, Bpt)
                rd, Brd = rdr.next()
                RECIP(rd, bank(db), [PB[db]], [Brd])
                TT("dve", AT[:, h, qg * 512:(qg + 1) * 512], bank(ob), rd, ALU.mult, [PB[ob], Brd], [B_AT[qg]])
        if stage == "A":
            P.barrier()
            AR.release("KAT", "VA")
            dtmp = AR.alloc("dtmp", [128, 8 * NOWN], F32)
            B_dtmp = Buf("dtmp")
            CP("dve", dtmp, AT.rearrange("p a n -> p (a n)"), B_AT, [B_dtmp])
            P.dma("sp", dbg, dtmp, B_dtmp, False)
            P.barrier()
            P.emit(st)
            return nc
        P.barrier()
        AR.release("KAT", "VA", "QAT")
        ptr.release()
        rdr.release()

        Wg = AR.alloc("Wg", [128, 8, 2048], BF16)
        WoA = AR.alloc("WoA", [128, 8, D], BF16)
        WoB = AR.alloc("WoB", [128, 2, D], BF16)
        B_Wg, B_WoA, B_WoB = Buf("Wg"), Buf("WoA"), Buf("WoB")
        LOADW(Wg, w_in_v[:, :, 3840:5888], B_Wg)
        LOADW(WoA, w_oa.rearrange("(k p) n -> p k n", p=128), B_WoA)
        LOADW(WoB, w_ob.rearrange("(k p) n -> p k n", p=128), B_WoB)
        xring = Ring("xr", 2, [128, D], F32)
        hbring = Ring("hb", 2, [128, D], BF16, top=True)
        hTg = Ring("hTg", 2, [128, 8, 512], BF16)
        Gtmp = AR.alloc("Gtmp", [128, 8, 512], BF16)
        B_Gtmp = Buf("Gtmp")
        sar = Ring("sa", 2, [128, 512], F32)
        sbr = Ring("sb", 2, [128, 512], F32)
        m1r = Ring("m1", 2, [128, 512], F32)
        m2r = Ring("m2", 2, [128, 512], F32)
        for grp in range(4):
            hT, BhT = hTg.next()
            for t in range(4):
                T = grp * 4 + t
                xa, Bx = xring.next()
                P.dma("sp", xa, xs[T * 128:(T + 1) * 128, :], Bx, True)
                norm_to_hT(xa, Bx, hbring, 0, hT[:, :, t * 128:(t + 1) * 128], BhT, "act")
            tok = slice(grp * 512, (grp + 1) * 512)
            for c in range(8):
                a = c % 2
                bga, bgb, bya, byb = 1 + a, 7, 3 + a, 5 + a
                cs = slice(c * 128, (c + 1) * 128)
                for k in range(8):
                    MM(bank(bga), Wg[:, k, cs], hT[:, k, :], k == 0, k == 7, [B_Wg, BhT], [PB[bga]])
                sa, Bsa = sar.next()
                ACTV(sa, bank(bga), AF.Sigmoid, [PB[bga], B_bgate], [Bsa], bias=bgate[:, c:c + 1])
                for k in range(8):
                    MM(bank(bgb), Wg[:, k, 1024 + c * 128:1024 + (c + 1) * 128], hT[:, k, :], k == 0, k == 7,
                       [B_Wg, BhT], [PB[bgb]])
                sb_, Bsb = sbr.next()
                ACTV(sb_, bank(bgb), AF.Sigmoid, [PB[bgb], B_bgate], [Bsb], bias=bgate[:, 8 + c:9 + c])
                for k in range(8):
                    MM(bank(bya), WoA[:, k, cs], AT[:, k, tok], k == 0, k == 7, [B_WoA, B_AT[grp]], [PB[bya]])
                for k in range(2):
                    MM(bank(byb), WoB[:, k, cs], BT[:, k, tok], k == 0, k == 1, [B_WoB, B_BT], [PB[byb]])
                m1, Bm1 = m1r.next()
                m2, Bm2 = m2r.next()
                TT("dve", m1, bank(bya), sa, ALU.mult, [PB[bya], Bsa], [Bm1])
                TT("dve", m2, bank(byb), sb_, ALU.mult, [PB[byb], Bsb], [Bm2])
                TT("pool", Gtmp[:, c, :], m1, m2, ALU.add, [Bm1, Bm2], [B_Gtmp])
            CP("pool", AT[:, :, tok], Gtmp, [B_Gtmp], [B_AT[grp]])
        P.barrier()
        AR.release("Wg", "WoA", "WoB", "Gtmp", "BT")
        for r_ in (xring, hTg, sar, sbr, m1r, m2r):
            r_.release()

        R = AR.alloc("R", [128, 16, D], F32)
        B_R = [Buf("R%d" % i) for i in range(16)]
        WO = AR.alloc("WO", [128, 8, D], BF16)
        B_WO = Buf("WO")
        LOADW(WO, w_o.rearrange("(k p) n -> p k n", p=128), B_WO)
        for T in range(16):
            P.dma("sp", R[:, T, :], xs[T * 128:(T + 1) * 128, :], B_R[T], True)
        for T in range(16):
            for hf in range(2):
                bk = (2 * T + hf) % 4
                for k in range(8):
                    MM(bank(bk), AT[:, k, T * 128:(T + 1) * 128], WO[:, k, hf * 512:(hf + 1) * 512], k == 0, k == 7,
                       [B_AT[T // 4], B_WO], [PB[bk]])
                dst = R[:, T, hf * 512:(hf + 1) * 512]
                TT("dve", dst, bank(bk), dst, ALU.add, [PB[bk], B_R[T]], [B_R[T]])
        P.barrier()
        AR.release("WO", "AT")

        hT2 = AR.alloc("hT2", [128, 8, NOWN], BF16)
        B_hT2 = [Buf("hT2_%d" % i) for i in range(8)]
        P.dma("sp", gbc, g_mlp.partition_broadcast(128), B_gbc, True)
        W1r = Ring("W1q", 2, [128, 8, 1024], BF16)
        W2r = Ring("W2q", 2, [128, 8, 1024], BF16)
        w_f1_v = w_f1.rearrange("(k p) n -> p k n", p=128)
        w_f2_v = w_f2.rearrange("(k p) n -> p k n", p=128)
        wq = []
        for q_ in range(2):
            w1, Bw1 = W1r.next()
            w2, Bw2 = W2r.next()
            LOADW(w1, w_f1_v[:, :, q_ * 1024:(q_ + 1) * 1024], Bw1)
            LOADW(w2, w_f2_v[:, q_ * 8:(q_ + 1) * 8, :], Bw2)
            wq.append((w1, Bw1, w2, Bw2))
        lockstep([norm_g(R[:, T, :], B_R[T], T % 2, hT2[:, :, T * 128:(T + 1) * 128], B_hT2[T // 2])
                  for T in range(16)], 2)
        uTr = Ring("uT", 3, [128, 8, 256], BF16)
        rlr = Ring("rl", 3, [128, 256], F32)
        ucnt = [0]

        def u_part(w1, Bw1, tg):
            uT, BuT = uTr.next()
            for j in range(8):
                bk = ucnt[0] % 4
                ucnt[0] += 1
                for k in range(8):
                    MM(bank(bk, 256), w1[:, k, j * 128:(j + 1) * 128], hT2[:, k, tg * 256:(tg + 1) * 256],
                       k == 0, k == 7, [Bw1, B_hT2[tg]], [PB[bk]])
                rl, Brl = rlr.next()
                ACTV(rl, bank(bk, 256), AF.Relu, [PB[bk]], [Brl])
                TT("pool" if j % 2 else "dve", uT[:, j, :], rl, rl, ALU.mult, [Brl], [BuT])
            return uT, BuT

        def y_part(w2, Bw2, tg, uT, BuT):
            for t in range(2):
                T = tg * 2 + t
                for hf in range(2):
                    bk = 4 + t * 2 + hf
                    for j in range(8):
                        MM(bank(bk), uT[:, j, t * 128:(t + 1) * 128], w2[:, j, hf * 512:(hf + 1) * 512],
                           j == 0, j == 7, [BuT, Bw2], [PB[bk]])
                    dst = R[:, T, hf * 512:(hf + 1) * 512]
                    TT("dve", dst, bank(bk), dst, ALU.add, [PB[bk], B_R[T]], [B_R[T]])

        work = [(q_, tg) for q_ in range(4) for tg in range(8)]
        prev = None
        for (q_, tg) in work:
            if q_ >= 2 and tg == 0:
                w1, Bw1 = W1r.next()
                w2, Bw2 = W2r.next()
                LOADW(w1, w_f1_v[:, :, q_ * 1024:(q_ + 1) * 1024], Bw1)
                LOADW(w2, w_f2_v[:, q_ * 8:(q_ + 1) * 8, :], Bw2)
                wq.append((w1, Bw1, w2, Bw2))
            w1, Bw1, w2, Bw2 = wq[q_]
            cur = (w2, Bw2, tg) + u_part(w1, Bw1, tg)
            if prev is not None:
                y_part(*prev)
            prev = cur
        y_part(*prev)
        P.barrier()
        W1r.release()
        W2r.release()
        uTr.release()
        rlr.release()

        P.dma("sp", gbc, g_ple.partition_broadcast(128), B_gbc, True)
        Wpg = AR.alloc("Wpg", [128, 8, D], BF16)
        Wp = AR.alloc("Wp", [128, 2, D], BF16)
        B_Wpg, B_Wp = Buf("Wpg"), Buf("Wp")
        LOADW(Wpg, w_pg.rearrange("(k p) n -> p k n", p=128), B_Wpg)
        LOADW(Wp, w_p.rearrange("(k p) n -> p k n", p=128), B_Wp)
        pT = AR.alloc("pT", [128, 2, NOWN], BF16)
        B_pT = [Buf("pT%d" % i) for i in range(16)]
        ppr = Ring("ppf", 2, [128, 256], F32)
        pbr = Ring("ppb", 2, [128, 256], BF16)
        def tile4d_g(T):
            pf, Bpf = ppr.next()
            P.dma("sp", pf, pp[T * 128:(T + 1) * 128, :], Bpf, True)
            yield
            for _ in norm_g(R[:, T, :], B_R[T], T % 2, hT2[:, :, T * 128:(T + 1) * 128], B_hT2[T // 2]):
                yield
            pb, Bpb = pbr.next()
            CP("pool", pb, pf, [Bpf], [Bpb])
            yield
            ptp = bankb(2 + T % 2)
            for k in range(2):
                TR(ptp[:, k * 128:(k + 1) * 128], pb[:, k * 128:(k + 1) * 128], [Bpb], [PB[2 + T % 2]])
            yield
            CP("dve", pT[:, :, T * 128:(T + 1) * 128], ptp[:, 0:256].rearrange("p (k n) -> p k n", k=2),
               [PB[2 + T % 2]], [B_pT[T]])
            yield

        lockstep([tile4d_g(T) for T in range(16)], 2)
        sgr = Ring("sg", 2, [128, 512], F32)
        mpr = Ring("mp", 2, [128, 512], F32)
        for T in range(16):
            for hf in range(2):
                a = (2 * T + hf) % 2
                bg_, be_ = 4 + a, 6 + a
                cs = slice(hf * 512, (hf + 1) * 512)
                for k in range(8):
                    MM(bank(bg_), hT2[:, k, T * 128:(T + 1) * 128], Wpg[:, k, cs], k == 0, k == 7,
                       [B_hT2[T // 2], B_Wpg], [PB[bg_]])
                for k in range(2):
                    MM(bank(be_), pT[:, k, T * 128:(T + 1) * 128], Wp[:, k, cs], k == 0, k == 1,
                       [B_pT[T], B_Wp], [PB[be_]])
                sg, Bsg = sgr.next()
                ACTV(sg, bank(bg_), AF.Sigmoid, [PB[bg_]], [Bsg])
                mp, Bmp = mpr.next()
                TT("dve", mp, bank(be_), sg, ALU.mult, [PB[be_], Bsg], [Bmp])
                dst = R[:, T, cs]
                TT("pool", dst, dst, mp, ALU.add, [Bmp, B_R[T]], [B_R[T]])
        gfin = AR.alloc("gfin", [128, D], F32)
        B_gfin = Buf("gfin")
        P.dma("sp", gfin, g_fin.partition_broadcast(128), B_gfin, True)
        for T in range(16):
            rs, Brs = rms_rstd(R[:, T, :], B_R[T], D)
            STT("dve", R[:, T, :], R[:, T, :], rs[:, 0:1], gfin, ALU.mult, ALU.mult, [B_R[T], Brs, B_gfin], [B_R[T]])
            P.dma("sp", out[T * 128:(T + 1) * 128, :], R[:, T, :], B_R[T], False)
        P.barrier()
        P.emit(st)
    return nc


def _t5_bucket(rel):
    nb = 16
    ret = (rel > 0).astype(np.int32) * nb
    n = np.abs(rel)
    max_exact = nb // 2
    large = max_exact + (np.log(np.maximum(n, 1) / max_exact) / math.log(1024 / max_exact)
                         * (nb - max_exact)).astype(np.int32)
    large = np.minimum(large, nb - 1)
    return ret + np.where(n < max_exact, n, large).astype(np.int32)


def _rope_table():
    half = 64
    inv_freq = np.power(np.float32(10000.0), -np.arange(0, half, 2, dtype=np.float32) / np.float32(half)).astype(np.float32)
    pos = np.arange(S)
    row = (pos // 64).astype(np.float32)
    col = (pos % 64).astype(np.float32)
    ar = (row[:, None] * inv_freq[None, :]).astype(np.float32)
    ac = (col[:, None] * inv_freq[None, :]).astype(np.float32)
    cr, sr, cc, sc = np.cos(ar), np.sin(ar), np.cos(ac), np.sin(ac)
    C = np.concatenate([cr, cr, cc, cc], axis=1)
    Sg = np.concatenate([-sr, sr, -sc, sc], axis=1)
    return np.concatenate([C, Sg], axis=1).astype(np.float32)


def _bias_tables(rel_bias):
    p = np.arange(128)[:, None]
    q = np.arange(128)[None, :]
    tabs = np.empty((6, 128, 512), np.float32)
    for g, c in enumerate(DIL):
        for kind in range(2):
            off = p - 64 + 128 * kind - q
            bucket = _t5_bucket(off * c)
            band = np.abs(off) <= 64
            for hh in range(4):
                vals = rel_bias[bucket, 4 * g + hh]
                tabs[2 * g + kind][:, hh * 128:(hh + 1) * 128] = np.where(band, vals, np.float32(NEG))
    return tabs


def _kmask(r0):
    km = np.zeros((128, 69), np.float32)
    p = np.arange(128)
    for g, c in enumerate(DIL):
        for r in range(c):
            for j in range(NJ[g]):
                th = 1024 + r - 64 * c + 128 * c * j + c * p
                ab = r0 - 1024 + th
                valid = (ab >= 0) & (ab < S)
                km[:, GOFF[g] + r * NJ[g] + j] = np.where(valid, 0.0, NEG)
    return km


def make_in_maps(inputs):
    f = lambda a: np.ascontiguousarray(np.asarray(a, dtype=np.float32))
    x = f(inputs["x"])
    p = f(inputs["p"])
    rope = _rope_table()
    biasT = _bias_tables(f(inputs["rel_bias"]))
    shared = {
        "biasT": biasT,
        "w_in": f(inputs["w_in"][0]), "w_out_a": f(inputs["w_out_a"][0]), "w_out_b": f(inputs["w_out_b"][0]),
        "w_out": f(inputs["w_out"][0]), "w_ff1": f(inputs["w_ff1"][0]), "w_ff2": f(inputs["w_ff2"][0]),
        "w_ple_gate": f(inputs["w_ple_gate"][0]), "w_ple": f(inputs["w_ple"][0]),
        "g_mix": f(inputs["norm_mix_g"][0]).reshape(1, D), "g_mlp": f(inputs["norm_mlp_g"][0]).reshape(1, D),
        "g_ple": f(inputs["norm_ple_g"][0]).reshape(1, D), "g_fin": f(inputs["final_norm_g"]).reshape(1, D),
        "g_q": f(inputs["q_norm_g"][0]).reshape(1, 128), "g_k": f(inputs["k_norm_g"][0]).reshape(1, 128),
        "bgate": f(f(inputs["b_gate"][0]).reshape(16, 128).T),
    }
    maps = []
    for core in range(8):
        b, r0 = core // 4, (core % 4) * NOWN
        xsr = np.ascontiguousarray(np.roll(x[b], -r0, axis=0))
        xhh = np.zeros((4096, D), np.float32)
        lo, hi = r0 - 1024, r0 + 3072
        a0, a1 = max(lo, 0), min(hi, S)
        xhh[a0 - lo:a1 - lo] = x[b, a0:a1]
        m = dict(shared)
        m.update({
            "xs": xsr, "xh": xhh, "pp": np.ascontiguousarray(p[0, b, r0:r0 + NOWN]),
            "rope": np.ascontiguousarray(np.roll(rope, -r0, axis=0)), "kmask": _kmask(r0),
        })
        maps.append(m)
    return maps


_NC_CACHE = {}


def kernel(**inputs):
    if "full" not in _NC_CACHE:
        _NC_CACHE["full"] = build("full")
    nc = _NC_CACHE["full"]
    maps = make_in_maps(inputs)
    res = run_bass_kernel_spmd(nc, maps, core_ids=list(range(8)))
    o = np.empty((2, S, D), np.float32)
    for core in range(8):
        b, r0 = core // 4, (core % 4) * NOWN
        o[b, r0:r0 + NOWN] = res.results[core]["out"]
    return o
```
